# Optimizing a Trainium2 kernel written in Bass

```python
import math
import jax, jax.numpy as jnp
from jax import lax
import numpy as np

D_MODEL = 2048
BATCH = 32
SEQ = 256
DEPTH = 4
DEC_BATCH = 2
DEC_SEQ = 2048
PAST_LEN = 512

GRID_W = 64
HEAD_DIM = 128
N_HEADS = 8
N_KV_HEADS = 2
KV_GROUP = N_HEADS // N_KV_HEADS
D_ATTN = N_HEADS * HEAD_DIM
D_KV = N_KV_HEADS * HEAD_DIM
WINDOW = 128
Q_BLOCK = 128
ROPE_THETA = 10000.0
D_LRU = D_MODEL // 4
LRU_BLOCKS = 4
LRU_BLK = D_LRU // LRU_BLOCKS
CONV_W = 4
CONV_LEFT = 2
LRU_C = 8.0
D_POOL = D_MODEL // 4
POOL_WINDOWS = (2, 4, 8, 16)
N_POOL_GROUPS = len(POOL_WINDOWS)
POOL_GROUP = D_POOL // N_POOL_GROUPS
D_MIX = D_ATTN + D_LRU + D_POOL
D_IN = D_ATTN + 2 * D_KV + 2 * D_LRU + D_POOL
SPLITS = (D_ATTN, D_ATTN + D_KV, D_ATTN + 2 * D_KV, D_ATTN + 2 * D_KV + D_LRU, D_ATTN + 2 * D_KV + 2 * D_LRU)
D_FF = -(-(8 * D_MODEL) // (3 * 256)) * 256
N_MOD = 6
RMS_EPS = 1e-6
MOD_INIT = 0.5
NEG_INF = -1e30

kernel_name = 'hymba_diffusion_prefix_step'


def _rmsnorm(x, g):
    xf = x.astype(jnp.float32)
    y = xf * lax.rsqrt(jnp.mean(xf * xf, axis=-1, keepdims=True) + RMS_EPS)
    return (y * g.astype(jnp.float32)).astype(x.dtype)


def _mod_vectors(cond, w, b):
    m = jax.nn.silu(cond) @ w + b
    m = m.reshape(cond.shape[0], N_MOD, D_MODEL)
    return jnp.moveaxis(m, 1, 0)[:, :, None, :]


def _rope_2d_tables(T, dtype):
    rows = T // GRID_W
    row = jnp.repeat(jnp.arange(rows), GRID_W)
    col = jnp.tile(jnp.arange(GRID_W), rows)
    pos = jnp.stack([row, col], axis=-1).astype(jnp.float32)
    rd = HEAD_DIM // 4
    inv = ROPE_THETA ** (-jnp.arange(rd, dtype=jnp.float32) / rd)
    ang = jnp.broadcast_to(pos[:, :, None, None] * inv, (T, 2, 2, rd)).reshape(T, HEAD_DIM)
    return jnp.cos(ang).astype(dtype), jnp.sin(ang).astype(dtype)


def _apply_rope(x, cos, sin):
    xr = x.reshape(x.shape[:-1] + (2, 2, HEAD_DIM // 4))
    rot = jnp.stack([-xr[..., 1, :], xr[..., 0, :]], axis=-2).reshape(x.shape)
    return x * cos[None, :, None, :] + rot * sin[None, :, None, :]


def _attend(q, k, v, bias, sink):
    s = jnp.einsum('bqkgd,bskd->bkgqs', q, k).astype(jnp.float32) * (HEAD_DIM ** -0.5) + bias
    B, Q = q.shape[0], q.shape[1]
    sink_col = jnp.broadcast_to(sink.astype(jnp.float32)[None, :, :, None, None], (B, N_KV_HEADS, KV_GROUP, Q, 1))
    p = jax.nn.softmax(jnp.concatenate([s, sink_col], axis=-1), axis=-1)[..., :-1]
    return jnp.einsum('bkgqs,bskd->bqkgd', p.astype(v.dtype), v)


def _context_attention(q, k, v, sink):
    B, C = q.shape[:2]
    bias = jnp.zeros((Q_BLOCK, C), jnp.float32)

    def blk(i):
        qb = lax.dynamic_slice_in_dim(q, i * Q_BLOCK, Q_BLOCK, axis=1)
        return _attend(qb, k, v, bias, sink)

    o = lax.map(blk, jnp.arange(C // Q_BLOCK))
    return jnp.moveaxis(o, 0, 1).reshape(B, C, D_ATTN)


def _latent_attention(q, k, v, ctx_k, ctx_v, sink):
    B, T = q.shape[:2]
    C = ctx_k.shape[1]
    band = Q_BLOCK + 2 * WINDOW
    kp = jnp.pad(k, ((0, 0), (WINDOW, WINDOW), (0, 0), (0, 0)))
    vp = jnp.pad(v, ((0, 0), (WINDOW, WINDOW), (0, 0), (0, 0)))
    rel = jnp.arange(Q_BLOCK)[:, None] - jnp.arange(band)[None, :] + WINDOW
    ctx_bias = jnp.zeros((Q_BLOCK, C), jnp.float32)

    def blk(i):
        start = i * Q_BLOCK
        qb = lax.dynamic_slice_in_dim(q, start, Q_BLOCK, axis=1)
        kb = lax.dynamic_slice_in_dim(kp, start, band, axis=1)
        vb = lax.dynamic_slice_in_dim(vp, start, band, axis=1)
        kpos = start - WINDOW + jnp.arange(band)
        valid = (jnp.abs(rel) <= WINDOW) & ((kpos >= 0) & (kpos < T))[None, :]
        bias = jnp.concatenate([jnp.where(valid, 0.0, NEG_INF).astype(jnp.float32), ctx_bias], axis=1)
        return _attend(qb, jnp.concatenate([kb, ctx_k], axis=1), jnp.concatenate([vb, ctx_v], axis=1), bias, sink)

    o = lax.map(blk, jnp.arange(T // Q_BLOCK))
    return jnp.moveaxis(o, 0, 1).reshape(B, T, D_ATTN)


def _dwconv(u, w, b):
    T = u.shape[1]
    up = jnp.pad(u, ((0, 0), (CONV_LEFT, CONV_W - 1 - CONV_LEFT), (0, 0)))
    return sum(up[:, j:j + T] * w[j] for j in range(CONV_W)) + b


def _lin_combine(left, right):
    a1, b1 = left
    a2, b2 = right
    return a1 * a2, a2 * b1 + b2


def _rglru_bidir(u, wa, ba, wx, bx, lam, h0):
    B, T, _ = u.shape
    ub = u.reshape(B, T, LRU_BLOCKS, LRU_BLK)
    uf = u.astype(jnp.float32)
    y = jnp.zeros((B, T, D_LRU), jnp.float32)
    finals = []
    for d in range(2):
        r = jax.nn.sigmoid((jnp.einsum('btnc,ncd->btnd', ub, wa[d]).reshape(B, T, D_LRU) + ba[d]).astype(jnp.float32))
        ig = jax.nn.sigmoid((jnp.einsum('btnc,ncd->btnd', ub, wx[d]).reshape(B, T, D_LRU) + bx[d]).astype(jnp.float32))
        log_a = LRU_C * r * jax.nn.log_sigmoid(lam[d].astype(jnp.float32))
        a = jnp.exp(log_a)
        bterm = jnp.sqrt(-jnp.expm1(2.0 * log_a)) * ig * uf
        if d == 1:
            a, bterm = a[:, ::-1], bterm[:, ::-1]
        a_cum, b_cum = lax.associative_scan(_lin_combine, (a, bterm), axis=1)
        h = a_cum * h0[:, d, None, :].astype(jnp.float32) + b_cum
        finals.append(h[:, -1])
        if d == 1:
            h = h[:, ::-1]
        y = y + h
    return y.astype(u.dtype), jnp.stack(finals, axis=1).astype(u.dtype)


def _pool_mixer(u, w, scale):
    B, T, _ = u.shape
    ug = u.reshape(B, T, N_POOL_GROUPS, POOL_GROUP).astype(jnp.float32)
    cs = jnp.pad(jnp.cumsum(ug, axis=1), ((0, 0), (1, 0), (0, 0), (0, 0)))
    t = jnp.arange(T)
    groups = []
    for g, win in enumerate(POOL_WINDOWS):
        lo = jnp.clip(t - win // 2, 0, T)
        hi = jnp.clip(t + win // 2, 0, T)
        cs_g = cs[:, :, g]
        mean = (cs_g[:, hi] - cs_g[:, lo]) / (hi - lo).astype(jnp.float32)[None, :, None]
        groups.append(mean - ug[:, :, g])
    pooled = jnp.stack(groups, axis=2).astype(u.dtype)
    y = jnp.einsum('btgc,gcd->btgd', pooled, w).reshape(B, T, D_POOL)
    return y * scale


def _mixer_inputs(x, mod, p):
    h = _rmsnorm(x, p['norm_mix']) * (1 + mod[1]) + mod[0]
    return jnp.split(h @ p['w_in'], SPLITS, axis=-1)


def _layer_tail(x, mod, attn, lru_h, lg, pool, p):
    mix = jnp.concatenate([attn, lru_h * jax.nn.gelu(lg), pool], axis=-1) @ p['w_out']
    x = x + mod[2] * mix
    h = _rmsnorm(x, p['norm_ffn']) * (1 + mod[4]) + mod[3]
    ga, up = jnp.split(h @ p['ffn_w1'], 2, axis=-1)
    return x + mod[5] * ((jax.nn.silu(ga) * up) @ p['ffn_w2'])


def _context_layer(x, mod, p):
    B, C, _ = x.shape
    q, k, v, lx, lg, pu = _mixer_inputs(x, mod, p)
    q = q.reshape(B, C, N_KV_HEADS, KV_GROUP, HEAD_DIM)
    k = k.reshape(B, C, N_KV_HEADS, HEAD_DIM)
    v = v.reshape(B, C, N_KV_HEADS, HEAD_DIM)
    attn = _context_attention(q, k, v, p['sink'])
    u = _dwconv(lx, p['conv_w'], p['conv_b'])
    h0 = jnp.zeros((B, 2, D_LRU), x.dtype)
    lru_h, h_fin = _rglru_bidir(u, p['lru_wa'], p['lru_ba'], p['lru_wx'], p['lru_bx'], p['lru_lambda'], h0)
    pool = _pool_mixer(pu, p['pool_w'], p['pool_scale'])
    return _layer_tail(x, mod, attn, lru_h, lg, pool, p), k, v, h_fin


def _latent_layer(x, mod, ctx_k, ctx_v, ctx_h, p):
    B, T, _ = x.shape
    q, k, v, lx, lg, pu = _mixer_inputs(x, mod, p)
    cos, sin = _rope_2d_tables(T, x.dtype)
    q = _apply_rope(q.reshape(B, T, N_HEADS, HEAD_DIM), cos, sin).reshape(B, T, N_KV_HEADS, KV_GROUP, HEAD_DIM)
    k = _apply_rope(k.reshape(B, T, N_KV_HEADS, HEAD_DIM), cos, sin)
    v = v.reshape(B, T, N_KV_HEADS, HEAD_DIM)
    attn = _latent_attention(q, k, v, ctx_k, ctx_v, p['sink'])
    u = _dwconv(lx, p['conv_w'], p['conv_b'])
    lru_h, _ = _rglru_bidir(u, p['lru_wa'], p['lru_ba'], p['lru_wx'], p['lru_bx'], p['lru_lambda'], ctx_h)
    pool = _pool_mixer(pu, p['pool_w'], p['pool_scale'])
    return _layer_tail(x, mod, attn, lru_h, lg, pool, p)


def setup_inputs(seed: int = 0) -> dict:
    key = jax.random.key(seed)
    ks = jax.random.split(key, 32)
    f32 = jnp.float32

    def nrm(k, shape, s):
        return jax.random.normal(k, shape, f32) * s

    a_c = jax.random.uniform(ks[19], (DEPTH, 2, D_LRU), f32, 0.9, 0.999)
    sig = a_c ** (1.0 / LRU_C)
    return {
        'x_prompt': nrm(ks[0], (BATCH, SEQ, D_MODEL), 1.0),
        'x_sample': nrm(ks[1], (DEC_BATCH, DEC_SEQ, D_MODEL), 1.0),
        'cache_k': nrm(ks[2], (DEC_BATCH, DEPTH, PAST_LEN, N_KV_HEADS, HEAD_DIM), 1.0),
        'cache_v': nrm(ks[3], (DEC_BATCH, DEPTH, PAST_LEN, N_KV_HEADS, HEAD_DIM), 1.0),
        'state_lru': nrm(ks[4], (DEC_BATCH, DEPTH, 2, D_LRU), 0.5),
        'c': nrm(ks[5], (DEC_BATCH, D_MODEL), 1.0),
        'c_ctx': nrm(ks[6], (D_MODEL,), 1.0),
        'mod_w': nrm(ks[7], (DEPTH, D_MODEL, N_MOD * D_MODEL), MOD_INIT * D_MODEL ** -0.5),
        'mod_b': nrm(ks[8], (DEPTH, N_MOD * D_MODEL), 0.02),
        'norm_mix': 1.0 + nrm(ks[9], (DEPTH, D_MODEL), 0.02),
        'norm_ffn': 1.0 + nrm(ks[10], (DEPTH, D_MODEL), 0.02),
        'w_in': nrm(ks[11], (DEPTH, D_MODEL, D_IN), D_MODEL ** -0.5),
        'attn_sink': nrm(ks[12], (DEPTH, N_HEADS), 0.5),
        'conv_w': nrm(ks[13], (DEPTH, CONV_W, D_LRU), CONV_W ** -0.5),
        'conv_b': nrm(ks[14], (DEPTH, D_LRU), 0.02),
        'lru_wa': nrm(ks[15], (DEPTH, 2, LRU_BLOCKS, LRU_BLK, LRU_BLK), LRU_BLK ** -0.5),
        'lru_ba': nrm(ks[16], (DEPTH, 2, D_LRU), 0.02),
        'lru_wx': nrm(ks[17], (DEPTH, 2, LRU_BLOCKS, LRU_BLK, LRU_BLK), LRU_BLK ** -0.5),
        'lru_bx': nrm(ks[18], (DEPTH, 2, D_LRU), 0.02),
        'lru_lambda': jnp.log(sig) - jnp.log1p(-sig),
        'pool_w': nrm(ks[20], (DEPTH, N_POOL_GROUPS, POOL_GROUP, POOL_GROUP), POOL_GROUP ** -0.5),
        'pool_scale': 1.0 + nrm(ks[21], (DEPTH, D_POOL), 0.02),
        'w_out': nrm(ks[22], (DEPTH, D_MIX, D_MODEL), D_MIX ** -0.5),
        'ffn_w1': nrm(ks[23], (DEPTH, D_MODEL, 2 * D_FF), D_MODEL ** -0.5),
        'ffn_w2': nrm(ks[24], (DEPTH, D_FF, D_MODEL), D_FF ** -0.5),
        'norm_final': 1.0 + nrm(ks[25], (D_MODEL,), 0.02),
    }


def reference(x_prompt, x_sample, cache_k, cache_v, state_lru, c, c_ctx, mod_w, mod_b, norm_mix, norm_ffn, w_in, attn_sink, conv_w, conv_b, lru_wa, lru_ba, lru_wx, lru_bx, lru_lambda, pool_w, pool_scale, w_out, ffn_w1, ffn_w2, norm_final):
    xp = x_prompt
    xs = x_sample
    ks_new, vs_new, hs_new = [], [], []
    for l in range(DEPTH):
        p = {
            'norm_mix': norm_mix[l], 'norm_ffn': norm_ffn[l], 'w_in': w_in[l],
            'sink': attn_sink[l].reshape(N_KV_HEADS, KV_GROUP),
            'conv_w': conv_w[l], 'conv_b': conv_b[l],
            'lru_wa': lru_wa[l], 'lru_ba': lru_ba[l], 'lru_wx': lru_wx[l], 'lru_bx': lru_bx[l],
            'lru_lambda': lru_lambda[l], 'pool_w': pool_w[l], 'pool_scale': pool_scale[l],
            'w_out': w_out[l], 'ffn_w1': ffn_w1[l], 'ffn_w2': ffn_w2[l],
        }
        mod_ctx = _mod_vectors(c_ctx[None, :], mod_w[l], mod_b[l])
        xp, k_l, v_l, h_l = _context_layer(xp, mod_ctx, p)
        ks_new.append(k_l)
        vs_new.append(v_l)
        hs_new.append(h_l)
        mod_lat = _mod_vectors(c, mod_w[l], mod_b[l])
        xs = _latent_layer(xs, mod_lat, cache_k[:, l], cache_v[:, l], state_lru[:, l], p)
    y_prompt = _rmsnorm(xp, norm_final)
    y_sample = _rmsnorm(xs, norm_final)
    new_cache_k = jnp.stack(ks_new, axis=1)
    new_cache_v = jnp.stack(vs_new, axis=1)
    new_state_lru = jnp.stack(hs_new, axis=1)
    return (y_prompt, y_sample, new_cache_k, new_cache_v, new_state_lru)
```

```python
import numpy as np
import concourse.bass as bass
import concourse.mybir as mybir
from concourse.bass_utils import run_bass_kernel_spmd

F32 = mybir.dt.float32
BF16 = mybir.dt.bfloat16
AF = mybir.ActivationFunctionType
ALU = mybir.AluOpType
AX = mybir.AxisListType

P = 128
D = 2048
NCH = 16
T = 2048
TT = 512
NTT = 4
NBK = 16
DEPTH = 4
HD = 128
NH = 8
NKV = 2
NFF = 44
SCALE = float(HD) ** -0.5
EPS = 1e-6
NSLOT = 4
SLOT_COLS = 4096
NS_MOD, NS_IN, NS_OUT, NS_W1, NS_W2 = 48, 12, 8, 44, 22
NS_LAYER = NS_MOD + NS_IN + NS_OUT + NS_W1 + NS_W2
V_MODB, V_NMIX, V_NFFN, V_CONVW, V_CONVB, V_BA, V_BX, V_LAM, V_PSC, V_H0, V_SINK = (
    0, 96, 112, 128, 144, 148, 156, 164, 172, 176, 184)
V_LAYER = 192
V_NFIN = DEPTH * V_LAYER
V_FLAG = V_NFIN + 16
V_CTXB = V_FLAG + 1
NV = V_CTXB + 1

ENGS = ("pe", "dve", "act", "pool", "sp")


class Sched:
    def __init__(self):
        self.stream = {e: [] for e in ENGS}
        self.cnt = {e: 0 for e in ENGS}
        self.waited = {e: {} for e in ENGS}
        self.res = {}
        self.dval = {}

    def _need(self, eng, deps):
        wd = self.waited[eng]
        out = []
        for key, val in deps:
            if eng == "pe" and key == "pe":
                continue
            if wd.get(key, 0) >= val:
                continue
            wd[key] = val
            out.append((key, val))
        return out

    def op(self, eng, fn, reads=(), writes=(), dsem=None):
        deps = []
        for r in reads:
            st = self.res.get(r)
            if st is not None:
                if st[0] is not None:
                    deps.append(st[0])
                if isinstance(r, tuple) and r[0] == "ps":
                    deps.extend((k, v) for k, v in st[1].items() if k != eng)
        for w in writes:
            st = self.res.get(w)
            if st is not None:
                if st[0] is not None:
                    deps.append(st[0])
                deps.extend(st[1].items())
        if dsem is not None:
            prev = self.dval.get(dsem, 0)
            if prev:
                deps.append((dsem, prev))
            val = prev + 16
            self.dval[dsem] = val
            ev = (dsem, val)
        else:
            self.cnt[eng] += 1
            ev = (eng, self.cnt[eng])
        waits = self._need(eng, deps)
        self.stream[eng].append((waits, fn, dsem))
        for r in reads:
            st = self.res.get(r)
            if st is None:
                self.res[r] = [None, {ev[0]: ev[1]}]
            else:
                if st[1].get(ev[0], 0) < ev[1]:
                    st[1][ev[0]] = ev[1]
        for w in writes:
            self.res[w] = [ev, {}]
        return ev

    def barrier(self):
        deps = [(e, self.cnt[e]) for e in ("pe", "dve", "act", "pool") if self.cnt[e]]
        deps += [(k, v) for k, v in self.dval.items() if not k.startswith("d_ring")]
        for e in ENGS:
            waits = self._need(e, deps)
            if waits:
                self.stream[e].append((waits, None, None))
        self.res = {k: v for k, v in self.res.items() if isinstance(k, tuple) and k[0] == "ring"}


class Arena:
    def __init__(self, ap, ncols):
        self.ap = ap
        self.ncols = ncols
        self.off = 0

    def reset(self, off=0):
        self.off = off

    def _take(self, n4):
        assert self.off + n4 <= self.ncols, ("arena overflow", self.off, n4, self.ncols)
        v = self.ap[:, self.off:self.off + n4]
        self.off += n4
        return v

    @staticmethod
    def _shape(v, shape):
        if len(shape) == 1:
            return v
        if len(shape) == 2:
            return v.rearrange("p (a b) -> p a b", a=shape[0])
        if len(shape) == 3:
            return v.rearrange("p (a b c) -> p a b c", a=shape[0], b=shape[1])
        raise ValueError(shape)

    def f32(self, *shape):
        n = int(np.prod(shape))
        return self._shape(self._take(n), shape)

    def bf16(self, *shape):
        n = int(np.prod(shape))
        assert n % 2 == 0
        return self._shape(self._take(n // 2).bitcast(BF16), shape)


class _Stop(Exception):
    pass


STOP_AT = None


def build_program(nl=DEPTH, debug=False):
    _, plan = _build(nl, None)
    nc, plan2 = _build(nl, plan)
    assert plan2 == plan
    return nc


def _build(nl, plan_in):
    nc = bass.Bass("TRN2", target_bir_lowering=False)
    dt = nc.dram_tensor
    xT = dt("xT", [D, T], F32, kind="ExternalInput").ap()
    cond_d = dt("cond", [P, NCH], F32, kind="ExternalInput").ap()
    vecs_d = dt("vecs", [P, NV], F32, kind="ExternalInput").ap()
    wall = dt("wall", [nl * NS_LAYER, P, SLOT_COLS], F32, kind="ExternalInput").ap()
    lruw_d = dt("lruw", [DEPTH, P, 16 * 128], F32, kind="ExternalInput").ap()
    poolw_d = dt("poolw", [DEPTH, P, 4 * 128], F32, kind="ExternalInput").ap()
    ckT_d = dt("ckT", [DEPTH, P, NKV * 512], F32, kind="ExternalInput").ap()
    cv_d = dt("cv", [DEPTH, 512, NKV * HD], F32, kind="ExternalInput").ap()
    cos_d = dt("cosT", [P, T], F32, kind="ExternalInput").ap()
    sin_d = dt("sinT", [P, T], F32, kind="ExternalInput").ap()
    bias_d = dt("biasb", [P, NBK * 384], F32, kind="ExternalInput").ap()
    pint_d = dt("pint", [P, 4 * 112], F32, kind="ExternalInput").ap()
    pedge_d = dt("pedge", [P, NBK * 4 * 4 * 8], F32, kind="ExternalInput").ap()
    cmat_d = dt("cmat", [P, 3 * 128], F32, kind="ExternalInput").ap()

    yT = dt("yT", [D, T], F32, kind="ExternalOutput").ap()
    kT_o = dt("kT_o", [DEPTH, NKV, P, T], F32, kind="ExternalOutput").ap()
    v_o = dt("v_o", [DEPTH, T, NKV * HD], F32, kind="ExternalOutput").ap()
    st_o = dt("st_o", [DEPTH, 2, 4, P, 8], F32, kind="ExternalOutput").ap()
    xsA = dt("xsA", [D, T], F32, kind="Internal").ap()
    xsB = dt("xsB", [D, T], F32, kind="Internal").ap()

    S = Sched()
    ARENA_COLS = 42496
    PERS_COLS = 2 * 1024
    RING_COLS = NSLOT * SLOT_COLS // 2

    sem_names = ["pe", "dve", "act", "pool"] + ["d_ring%d" % i for i in range(NSLOT)]
    dyn_sems = ["d_ld%d" % i for i in range(6)] + ["d_st%d" % i for i in range(4)] + [
        "d_c%d" % i for i in range(12)] + ["d_h%d" % i for i in range(4)] + ["d_vo0", "d_vo1", "d_ko0", "d_ko1", "d_so"]
    sem_names += dyn_sems

    import contextlib
    with contextlib.ExitStack() as es:
        arena_t = es.enter_context(nc.sbuf_tensor("arena", [P, ARENA_COLS], F32))
        pers_t = es.enter_context(nc.sbuf_tensor("pers", [P, PERS_COLS], F32))
        ring_t = es.enter_context(nc.sbuf_tensor("ring", [P, NSLOT * SLOT_COLS], BF16))
        ps_t = es.enter_context(nc.psum_tensor("ps", [P, 8 * 512], F32))
        semh = {n: es.enter_context(nc.semaphore(n)) for n in sem_names}
        block = es.enter_context(nc.Block())

        ar = Arena(arena_t[:, :], ARENA_COLS)
        pr = Arena(pers_t[:, :], PERS_COLS)
        ring = [ring_t[:, i * SLOT_COLS:(i + 1) * SLOT_COLS] for i in range(NSLOT)]

        def bank(b):
            return ps_t[:, b * 512:(b + 1) * 512]

        def bank_bf(b):
            return ps_t[:, b * 512:(b + 1) * 512].bitcast(BF16)

        vecs = pr.f32(NV)
        cond = pr.f32(NCH)
        scb = pr.bf16(NCH)
        mods = pr.f32(DEPTH, 96)
        cm = pr.bf16(3, 128)
        ones_b, rot_b, ident_b = cm[:, 0, :], cm[:, 1, :], cm[:, 2, :]
        gs1 = pr.f32(NCH)
        gs2 = pr.f32(NCH)
        lam8 = pr.f32(8)
        cwe = pr.f32(16)
        sm = pr.f32(64)
        rstdF = None

        def DMA(q, out, in_, dsem, reads=(), writes=()):
            return S.op(q, lambda e: e.dma_start(out=out, in_=in_), reads, writes, dsem=dsem)

        def MM(out, lhsT, rhs, start, stop, reads=(), writes=()):
            for r_ in reads:
                if isinstance(r_, tuple) and r_[0] == "ring":
                    assert r_[1] == (ring_state["next"] - 1) % NSLOT, "stale ring slot"
            return S.op("pe", lambda e: e.matmul(out, lhsT, rhs, start=start, stop=stop), reads, writes)

        def TR(out, in_, reads=(), writes=()):
            return S.op("pe", lambda e: e.transpose(out, in_, ident_b), reads, writes)

        def ACT(out, in_, func, reads=(), writes=(), bias=None, scale=None):
            kw = {}
            if bias is not None:
                kw["bias"] = bias
            if scale is not None:
                kw["scale"] = scale
            return S.op("act", lambda e: e.activation(out, in_, func, **kw), reads, writes)

        def TT_(eng, out, in0, in1, op, reads=(), writes=()):
            return S.op(eng, lambda e: e.tensor_tensor(out, in0, in1, op), reads, writes)

        def TS(eng, out, in0, s1, s2, op0, op1=None, reads=(), writes=()):
            if op1 is None:
                return S.op(eng, lambda e: e.tensor_scalar(out, in0, s1, None, op0), reads, writes)
            return S.op(eng, lambda e: e.tensor_scalar(out, in0, s1, s2, op0, op1), reads, writes)

        def STT(out, in0, scalar, in1, op0, op1, reads=(), writes=()):
            return S.op("dve", lambda e: e.scalar_tensor_tensor(out, in0, scalar, in1, op0, op1), reads, writes)

        def CP(eng, out, in_, reads=(), writes=()):
            if eng == "act":
                return S.op("act", lambda e: e.copy(out, in_), reads, writes)
            return S.op(eng, lambda e: e.tensor_copy(out, in_), reads, writes)

        def MEMSET(eng, ap, val, writes=()):
            return S.op(eng, lambda e: e.memset(ap, val), (), writes)

        ring_state = {"issued": 0, "next": 0, "plan": list(plan_in) if plan_in is not None else [], "rec": []}

        def ring_issue_upto(k):
            plan = ring_state["plan"]
            while ring_state["issued"] < min(k, len(plan)):
                f = ring_state["issued"]
                slot = f % NSLOT
                src = wall[plan[f]].rearrange("p (a b) -> p a b", b=2048)
                dst = ring[slot].rearrange("p (a b) -> p a b", b=2048)
                DMA("pool", dst, src, "d_ring%d" % slot, writes=[("ring", slot)])
                ring_state["issued"] += 1

        def ring_next(idx):
            f = ring_state["next"]
            ring_state["next"] += 1
            ring_state["rec"].append(idx)
            if plan_in is None:
                ring_state["plan"].append(idx)
            else:
                assert ring_state["plan"][f] == idx
            ring_issue_upto(f + NSLOT)
            slot = f % NSLOT
            return ring[slot].rearrange("p (a b) -> p a b", b=128), ("ring", slot)

        pb = {"i": 0}

        def next_bank(lo=0, hi=8):
            b = lo + (pb["i"] % (hi - lo))
            pb["i"] += 1
            return b

        ld_rr = {"i": 0}

        DMA("sp", vecs, vecs_d, "d_c0", writes=["vecs"])
        DMA("sp", cond, cond_d, "d_c1", writes=["cond"])
        DMA("pool", cm.rearrange("p a b -> p (a b)"), cmat_d, "d_c2", writes=["cm"])
        ACT(scb, cond, AF.Silu, reads=["cond"], writes=["scb"])
        def mod_steps(l):
            mb = 7
            for nb in range(96):
                if nb % 2 == 0:
                    slot, rkey = ring_next(l * NS_LAYER + nb // 2)
                for kc in range(NCH):
                    MM(bank(mb)[:, nb:nb + 1], slot[:, (nb % 2) * 16 + kc, :], scb[:, kc:kc + 1],
                       kc == 0, kc == NCH - 1, reads=[rkey, "scb", "cm"], writes=[("ps", mb)])
                if nb % 2 == 1:
                    yield nb
            TT_("dve", mods[:, l, :], bank(mb)[:, 0:96], vecs[:, l * V_LAYER + V_MODB:l * V_LAYER + V_MODB + 96],
                ALU.add, reads=[("ps", mb), "vecs"], writes=[("mods", l)])

        for _ in mod_steps(0):
            pass

        PREFIX = 0

        def layer(l):
            vb = l * V_LAYER
            x_in = xT if l == 0 else xsA

            def vcol(off, n=1):
                return vecs[:, vb + off:vb + off + n]

            def mod(j):
                return mods[:, l, j * 16:(j + 1) * 16]

            STT(gs1, mod(1), 1.0, vcol(V_NMIX, 16), ALU.add, ALU.mult, reads=[("mods", l), "vecs"], writes=["gs1"])
            STT(gs2, mod(4), 1.0, vcol(V_NFFN, 16), ALU.add, ALU.mult, reads=[("mods", l), "vecs"], writes=["gs2"])
            ACT(lam8, vcol(V_LAM, 8), AF.Sigmoid, reads=["vecs"], writes=["lam8a"])
            ACT(lam8, lam8, AF.Ln, reads=["lam8a"], writes=["lam8b"])
            TS("dve", lam8, lam8, 8.0, None, ALU.mult, reads=["lam8b"], writes=["lam8"])
            TS("dve", cwe, vcol(V_CONVW, 16), vecs[:, V_FLAG:V_FLAG + 1], None, ALU.mult, reads=["vecs"], writes=["cwe"])

            S.barrier()
            ar.reset(0)
            MX = ar.bf16(16, T)
            kT = ar.bf16(NKV, 18 * 128)
            vS = ar.bf16(18, NKV, 132)
            lx = ar.bf16(4, T)
            prefix_off = ar.off
            cosT = ar.f32(T)
            sinT = ar.f32(T)
            hT = ar.bf16(NCH, TT)
            stg = [ar.f32(TT) for _ in range(4)]
            sq = [ar.bf16(TT) for _ in range(2)]
            tmpn = [ar.f32(TT) for _ in range(2)]
            rstd = ar.f32(TT)
            qb16 = [ar.bf16(TT) for _ in range(2)]
            qc = [ar.f32(TT) for _ in range(2)]
            qs = [ar.f32(TT) for _ in range(2)]
            kf = [ar.f32(TT) for _ in range(2)]
            vf = [ar.f32(256) for _ in range(2)]

            DMA("sp", cosT, cos_d, "d_c3", writes=["cos"])
            DMA("sp", sinT, sin_d, "d_c4", writes=["sin"])
            MEMSET("pool", kT[:, :, 0:128], 0.0, writes=[("kT", 0, -1), ("kT", 1, -1)])
            MEMSET("pool", kT[:, :, 17 * 128:18 * 128], 0.0, writes=[("kT", 0, 16), ("kT", 1, 16)])
            MEMSET("pool", vS[:, 0, :, :], 0.0, writes=[("vS", -1)])
            MEMSET("pool", vS[:, 17, :, :], 0.0, writes=[("vS", 16)])
            S.op("pool", lambda e: e.memset(vS[:, :, :, 128:129], 1.0), (), ["vones"] + [("vS", b_) for b_ in range(-1, 17)])

            def xload(c, tt, k):
                return DMA("sp", stg[k], x_in[c * P:(c + 1) * P, tt * TT:(tt + 1) * TT], "d_ld%d" % k,
                           reads=[("x", l, c, tt)], writes=[("stg", k)])

            for tt in range(NTT):
                tsl = slice(tt * TT, (tt + 1) * TT)
                sb_ = next_bank()
                for c in range(min(3, NCH)):
                    xload(c, tt, c % 4)
                for c in range(NCH):
                    if c + 3 < NCH:
                        xload(c + 3, tt, (c + 3) % 4)
                    ACT(sq[c % 2], stg[c % 4], AF.Square, reads=[("stg", c % 4)], writes=[("sq", c % 2)])
                    MM(bank(sb_), ones_b, sq[c % 2], c == 0, c == NCH - 1,
                       reads=[("sq", c % 2), "cm"], writes=[("ps", sb_)])
                ACT(rstd, bank(sb_), AF.Sqrt, reads=[("ps", sb_)], writes=["rstd_a"], bias=EPS, scale=1.0 / D)
                S.op("dve", lambda e: e.reciprocal(rstd, rstd), ["rstd_a"], ["rstd"])
                for c in range(min(3, NCH)):
                    xload(c, tt, c % 4)
                for c in range(NCH):
                    if c + 3 < NCH:
                        xload(c + 3, tt, (c + 3) % 4)
                    STT(tmpn[c % 2], stg[c % 4], gs1[:, c:c + 1], rstd, ALU.mult, ALU.mult,
                        reads=[("stg", c % 4), "gs1", "rstd"], writes=[("tmpn", c % 2)])
                    ACT(hT[:, c, :], tmpn[c % 2], AF.Identity, reads=[("tmpn", c % 2), ("mods", l)],
                        writes=[("hT", c)], bias=mod(0)[:, c:c + 1])
                hreads = [("hT", c) for c in range(NCH)]
                slot = rkey = None
                pend_a = [None]
                for cb in range(24):
                    if cb % 2 == 0:
                        slot, rkey = ring_next(l * NS_LAYER + NS_MOD + cb // 2)
                    if cb == 10:
                        for bi in range(4):
                            blk = tt * 4 + bi
                            b_ = next_bank()
                            for kc in range(NCH):
                                rhs = slot[:, kc:kc + 17:16, :]
                                MM(bank(b_)[:, 0:256], hT[:, kc, bi * P:(bi + 1) * P], rhs, kc == 0, kc == NCH - 1,
                                   reads=[rkey] + (hreads if kc == 0 else []), writes=[("ps", b_)])
                            if pend_a[0] is not None:
                                pend_a[0]()
                                pend_a[0] = None
                            k2 = blk % 2
                            CP("act", vf[k2], bank(b_)[:, 0:256], reads=[("ps", b_)], writes=[("vf", k2)])
                            CP("dve", vS[:, blk + 1, :, 0:128], bank(b_)[:, 0:256].rearrange("p (a b) -> p a b", a=2),
                               reads=[("ps", b_)], writes=[("vS", blk)])
                            DMA("sp", v_o[l, blk * P:(blk + 1) * P, :], vf[k2], "d_vo%d" % k2, reads=[("vf", k2)])
                        continue
                    if cb == 11:
                        continue
                    b_ = next_bank()
                    for kc in range(NCH):
                        MM(bank(b_), slot[:, (cb % 2) * 16 + kc, :], hT[:, kc, :], kc == 0, kc == NCH - 1,
                           reads=[rkey] + (hreads if kc == 0 else []), writes=[("ps", b_)])
                    if pend_a[0] is not None:
                        pend_a[0]()
                        pend_a[0] = None
                    if cb < 10:
                        i2 = cb % 2
                        CP("act", qb16[i2], bank(b_), reads=[("ps", b_)], writes=[("qb16", i2)])
                        TT_("dve", qc[i2], bank(b_), cosT[:, tsl], ALU.mult, reads=[("ps", b_), "cos"], writes=[("qc", i2)])

                        def rope_tail(cb=cb, i2=i2, tt=tt, tsl=tsl):
                            b2 = next_bank()
                            MM(bank(b2), rot_b, qb16[i2], True, True, reads=[("qb16", i2), "cm"], writes=[("ps", b2)])
                            TT_("dve", qs[i2], bank(b2), sinT[:, tsl], ALU.mult, reads=[("ps", b2), "sin"], writes=[("qs", i2)])
                            if cb < 8:
                                TT_("dve", MX[:, cb, tsl], qc[i2], qs[i2], ALU.add,
                                    reads=[("qc", i2), ("qs", i2)], writes=[("mx", cb, tt)])
                            else:
                                kv = cb - 8
                                TT_("dve", kf[kv], qc[i2], qs[i2], ALU.add,
                                    reads=[("qc", i2), ("qs", i2)], writes=[("kf", kv)])
                                CP("act", kT[:, kv, P + tt * TT:P + (tt + 1) * TT], kf[kv], reads=[("kf", kv)],
                                   writes=[("kT", kv, tt * 4 + j) for j in range(4)])
                                DMA("sp", kT_o[l, kv, :, tsl], kf[kv], "d_ko%d" % kv, reads=[("kf", kv)])

                        pend_a[0] = rope_tail
                    elif cb < 16:
                        CP("act", lx[:, cb - 12, tsl], bank(b_), reads=[("ps", b_)], writes=[("lx", cb - 12, tt)])
                    elif cb < 20:
                        ACT(MX[:, 8 + cb - 16, tsl], bank(b_), AF.Gelu_apprx_tanh, reads=[("ps", b_)],
                            writes=[("mx", 8 + cb - 16, tt)])
                    else:
                        CP("dve", MX[:, 12 + cb - 20, tsl], bank(b_), reads=[("ps", b_)], writes=[("mx", 12 + cb - 20, tt)])

            chk('A', l)
            S.barrier()
            ar.reset(prefix_off)
            biasb = ar.bf16(NBK, 384)
            ckT = ar.bf16(NKV, 512)
            cvS = ar.bf16(4, NKV, 132)
            pint = ar.bf16(4, 112)
            pedge = ar.bf16(NBK * 4 * 4, 8)
            poolw = ar.bf16(4, 128)
            zS = ar.bf16(NBK, 4 * 128)
            Sb = [ar.f32(384) for _ in range(2)]
            Pb = [ar.bf16(896) for _ in range(2)]
            PT = [ar.bf16(7, 128) for _ in range(2)]
            Osb = [ar.bf16(128) for _ in range(2)]
            DMA("pool", biasb.rearrange("p a b -> p (a b)").rearrange("p (a b) -> p a b", b=2048),
                bias_d.rearrange("p (a b) -> p a b", b=2048), "d_c5", writes=["biasb"])
            DMA("pool", ckT.rearrange("p a b -> p (a b)"), ckT_d[l], "d_c6", writes=["ckT"])
            for b4 in range(4):
                DMA("pool", cvS[:, b4, :, 0:128], cv_d[l, b4 * P:(b4 + 1) * P, :].rearrange("p (k d) -> p k d", k=NKV),
                    "d_c7", writes=[("cvSb", b4)])
            S.op("pool", lambda e: e.memset(cvS[:, :, :, 128:129], 1.0), (), ["cvones"])
            DMA("pool", pint.rearrange("p a b -> p (a b)"), pint_d, "d_c8", writes=["pint"])
            DMA("pool", pedge.rearrange("p a b -> p (a b)").rearrange("p (a b) -> p a b", b=2048),
                pedge_d.rearrange("p (a b) -> p a b", b=2048), "d_c9", writes=["pedge"])
            DMA("pool", poolw.rearrange("p a b -> p (a b)"), poolw_d[l], "d_c10", writes=["poolw"])

            items = [(qb, h) for qb in range(NBK) for h in range(NH)]
            nit = len(items)
            SP_ = [(0, 1), (2, 3)]
            PTB = 4
            OBS = [5, 6]
            OTB = 7
            st = vecs[:, vb + V_SINK:vb + V_SINK + 8]
            ctxb = vecs[:, V_CTXB:V_CTXB + 1]

            def smv(i, j):
                c0 = 8 * (i % 4) + j
                return sm[:, c0:c0 + 1]

            def phaseA(i):
                qb, h = items[i]
                kv = h // 4
                k2 = i % 2
                k4 = i % 4
                b0, b1 = SP_[k2]
                Sp = ps_t[:, b0 * 512:(b1 + 1) * 512]
                q_ap = MX[:, h, qb * P:(qb + 1) * P]
                MM(Sp[:, 128:512], q_ap, kT[:, kv, qb * P:(qb + 3) * P], True, True,
                   reads=[("mx", h, qb)] + [("kT", kv, j) for j in (qb - 1, qb, qb + 1)],
                   writes=[("ps", b0), ("ps", b1)])
                MM(Sp[:, 512:1024], q_ap, ckT[:, kv, :], True, True, reads=["ckT"], writes=[("ps", b1)])
                mx_, nb_, nb2, es_ = smv(i, 0), smv(i, 1), smv(i, 2), smv(i, 3)
                S.op("dve", lambda e: e.reduce_max(mx_, Sp[:, 128:1024], AX.X), [("ps", b0), ("ps", b1)], [("mx_", k4)])
                TS("dve", nb_, mx_, -SCALE, None, ALU.mult, reads=[("mx_", k4)], writes=[("nb", k4)])
                TS("dve", nb2, mx_, -SCALE, ctxb, ALU.mult, ALU.add, reads=[("mx_", k4), "vecs"], writes=[("nb2", k4)])
                TT_("dve", Sb[k2], Sp[:, 128:512], biasb[:, qb, :], ALU.add, reads=[("ps", b0), "biasb"], writes=[("Sb", k2)])
                ACT(Pb[k2][:, 0:384], Sb[k2], AF.Exp, reads=[("Sb", k2), ("nb", k4)], writes=[("Pb", k2)],
                    bias=nb_, scale=SCALE)
                ACT(Pb[k2][:, 384:896], Sp[:, 512:1024], AF.Exp, reads=[("ps", b1), ("nb2", k4)], writes=[("Pb", k2)],
                    bias=nb2, scale=SCALE)
                ACT(es_, st[:, h:h + 1], AF.Exp, reads=["vecs", ("nb", k4)], writes=[("es", k4)], bias=nb_, scale=1.0)

            def phaseT(i):
                k2 = i % 2
                ptp = bank_bf(PTB)
                for j in range(7):
                    TR(ptp[:, j * P:(j + 1) * P], Pb[k2][:, j * P:(j + 1) * P], reads=[("Pb", k2), "cm"], writes=[("ps", PTB)])
                CP("act", PT[k2].rearrange("p a b -> p (a b)"), ptp[:, 0:896], reads=[("ps", PTB)], writes=[("PT", k2)])

            def phaseV(i):
                qb, h = items[i]
                kv = h // 4
                k2 = i % 2
                k4 = i % 4
                ob = OBS[k2]
                O = bank(ob)[:, 0:129]
                for j in range(7):
                    if j < 3:
                        rhs = vS[:, qb + j, kv, 0:129]
                        rd = [("vS", qb + j - 1), "vones"]
                    else:
                        rhs = cvS[:, j - 3, kv, 0:129]
                        rd = [("cvSb", j - 3), "cvones"]
                    MM(O, PT[k2][:, j, :], rhs, j == 0, j == 6, reads=[("PT", k2)] + rd, writes=[("ps", ob)])
                den, es_ = smv(i, 4), smv(i, 3)
                TT_("dve", den, bank(ob)[:, 128:129], es_, ALU.add, reads=[("ps", ob), ("es", k4)], writes=[("den", k4)])
                S.op("dve", lambda e: e.reciprocal(den, den), [("den", k4)], [("rden", k4)])
                ACT(Osb[k2], bank(ob)[:, 0:128], AF.Copy, reads=[("ps", ob), ("rden", k4)], writes=[("Osb", k2)], scale=den)

            def phaseC(i):
                qb, h = items[i]
                k2 = i % 2
                otp = bank_bf(OTB)
                TR(otp[:, 0:128], Osb[k2], reads=[("Osb", k2), "cm"], writes=[("ps", OTB)])
                CP("dve", MX[:, h, qb * P:(qb + 1) * P], otp[:, 0:128], reads=[("ps", OTB)], writes=[("mx", h, qb)])

            for i in range(nit + 3):
                if i < nit:
                    phaseA(i)
                if 0 <= i - 1 < nit:
                    phaseT(i - 1)
                if 0 <= i - 2 < nit:
                    phaseV(i - 2)
                if 0 <= i - 3 < nit:
                    phaseC(i - 3)

            chk('attn', l)
            pb["i"] = 0
            for bi in range(NBK):
                b_ = next_bank(0, 4)
                for g in range(4):
                    MM(bank(b_)[:, g * P:(g + 1) * P], MX[:, 12 + g, bi * P:(bi + 1) * P], poolw[:, g, :], True, True,
                       reads=["poolw", ("mx", 12 + g, bi)], writes=[("ps", b_)])
                CP("act" if bi % 2 else "dve", zS[:, bi, :], bank(b_), reads=[("ps", b_)], writes=[("zS", bi)])
            pe4 = pedge.rearrange("p (i g k) e -> p i g k e", i=NBK, g=4, k=4)
            for g in range(4):
                for gi in range(4):
                    b_ = next_bank(4, 8)
                    for j in range(4):
                        bi = gi * 4 + j
                        o = bank(b_)[:, j * P:(j + 1) * P]
                        zi = zS[:, bi, g * P:(g + 1) * P]
                        MM(o[:, 8:120], zi, pint[:, g, :], True, True, reads=[("zS", bi), "pint"], writes=[("ps", b_)])
                        zp = zS[:, max(bi - 1, 0), g * P:(g + 1) * P]
                        zn = zS[:, min(bi + 1, NBK - 1), g * P:(g + 1) * P]
                        MM(o[:, 0:8], zp, pe4[:, bi, g, 0, :], True, False,
                           reads=[("zS", max(bi - 1, 0)), "pedge"], writes=[("ps", b_)])
                        MM(o[:, 0:8], zi, pe4[:, bi, g, 1, :], False, True, writes=[("ps", b_)])
                        MM(o[:, 120:128], zi, pe4[:, bi, g, 2, :], True, False, writes=[("ps", b_)])
                        MM(o[:, 120:128], zn, pe4[:, bi, g, 3, :], False, True,
                           reads=[("zS", min(bi + 1, NBK - 1))], writes=[("ps", b_)])
                    TS("dve", MX[:, 12 + g, gi * TT:(gi + 1) * TT], bank(b_), vcol(V_PSC + g), None, ALU.mult,
                       reads=[("ps", b_), "vecs"], writes=[("mx", 12 + g, "o", gi)])

            chk('B1', l)
            S.barrier()
            ar.reset(prefix_off)
            lw = ar.bf16(16, 128)
            u = ar.f32(T)
            ub = ar.bf16(T)
            aas = [ar.f32(T) for _ in range(2)]
            bbs = [ar.f32(T) for _ in range(2)]
            hf_ = ar.f32(T)
            hb_ = ar.f32(T)
            hh = [hf_, hb_]
            fin = ar.f32(2, 4, 8)
            DMA("pool", lw.rearrange("p a b -> p (a b)"), lruw_d[l], "d_c5", writes=["lw"])
            flag = vecs[:, V_FLAG:V_FLAG + 1]

            def seg(ap):
                return ap.rearrange("p (s t) -> p s t", t=256)

            for n in range(4):
                lxn = lx[:, n, :]
                w = lambda j: vcol(V_CONVW + j * 4 + n)
                we = lambda j: cwe[:, j * 4 + n:j * 4 + n + 1]
                TS("dve", u, lxn, w(2), vcol(V_CONVB + n), ALU.mult, ALU.add, reads=["vecs", "hsum"], writes=["u0"])
                us, ls = seg(u), seg(lxn)
                STT(us[:, :, 2:256], ls[:, :, 0:254], w(0), us[:, :, 2:256], ALU.mult, ALU.add, reads=["u0"], writes=["u1"])
                STT(us[:, :, 1:256], ls[:, :, 0:255], w(1), us[:, :, 1:256], ALU.mult, ALU.add, reads=["u1"], writes=["u2"])
                STT(us[:, :, 0:255], ls[:, :, 1:256], w(3), us[:, :, 0:255], ALU.mult, ALU.add, reads=["u2"], writes=["u3"])
                STT(us[:, 1:8, 0:2], ls[:, 0:7, 254:256], we(0), us[:, 1:8, 0:2], ALU.mult, ALU.add,
                    reads=["u3", "cwe"], writes=["u4"])
                STT(us[:, 1:8, 0:1], ls[:, 0:7, 255:256], we(1), us[:, 1:8, 0:1], ALU.mult, ALU.add, reads=["u4"], writes=["u5"])
                STT(us[:, 0:7, 255:256], ls[:, 1:8, 0:1], we(3), us[:, 0:7, 255:256], ALU.mult, ALU.add,
                    reads=["u5"], writes=["u"])
                CP("act", ub, u, reads=["u"], writes=["ub"])
                for d in range(2):
                    for tt in range(NTT):
                        tsl = slice(tt * TT, (tt + 1) * TT)
                        b1_ = next_bank()
                        MM(bank(b1_), lw[:, d * 4 + n, :], ub[:, tsl], True, True, reads=["lw", "ub"], writes=[("ps", b1_)])
                        b2_ = next_bank()
                        MM(bank(b2_), lw[:, 8 + d * 4 + n, :], ub[:, tsl], True, True, reads=["lw", "ub"], writes=[("ps", b2_)])
                        ACT(aas[d][:, tsl], bank(b1_), AF.Sigmoid, reads=[("ps", b1_), "vecs", ("scan", d)],
                            writes=[("aa", d, tt)], bias=vcol(V_BA + d * 4 + n))
                        ACT(bbs[d][:, tsl], bank(b2_), AF.Sigmoid, reads=[("ps", b2_), "vecs", ("scan", d)],
                            writes=[("bb", d, tt)], bias=vcol(V_BX + d * 4 + n))
                for d in range(2):
                    ACT(aas[d], aas[d], AF.Exp, reads=[("aa", d, t_) for t_ in range(NTT)] + ["lam8"], writes=[("aE", d)],
                        scale=lam8[:, d * 4 + n:d * 4 + n + 1])
                    TT_("dve", hh[d], aas[d], aas[d], ALU.mult, reads=[("aE", d), "hsum"], writes=[("a2", d)])
                for d in range(2):
                    ACT(hh[d], hh[d], AF.Sqrt, reads=[("a2", d)], writes=[("sq", d)], bias=1.0, scale=-1.0)
                for d in range(2):
                    TT_("dve", bbs[d], bbs[d], hh[d], ALU.mult, reads=[("bb", d, t_) for t_ in range(NTT)] + [("sq", d)],
                        writes=[("b1", d)])
                    TT_("dve", bbs[d], bbs[d], u, ALU.mult, reads=[("b1", d), "u"], writes=[("b2", d)])
                    asg = seg(aas[d])
                    h0 = vcol(V_H0 + d * 4 + n)
                    if d == 0:
                        TS("dve", asg[:, 1:8, 0:1], asg[:, 1:8, 0:1], flag, None, ALU.mult, reads=[("aE", 0), "vecs"], writes=[("aam", 0)])
                        S.op("dve", lambda e, h0=h0: e.tensor_tensor_scan(hf_, aas[0], bbs[0], h0, ALU.mult, ALU.add),
                             [("aam", 0), "vecs", ("b2", 0), ("sq", 0)], [("scan", 0), "hf"])
                        CP("act", fin[:, 0, n, :], seg(hf_)[:, :, 255], reads=["hf"], writes=[("fin", 0, n)])
                    else:
                        TS("dve", asg[:, 0:7, 255:256], asg[:, 0:7, 255:256], flag, None, ALU.mult,
                           reads=[("aE", 1), "vecs"], writes=[("aam", 1)])
                        S.op("dve", lambda e, h0=h0: e.tensor_tensor_scan(hb_[:, ::-1], aas[1][:, ::-1], bbs[1][:, ::-1], h0,
                                                                          ALU.mult, ALU.add),
                             [("aam", 1), "vecs", ("b2", 1), ("sq", 1)], [("scan", 1), "hb"])
                        CP("act", fin[:, 1, n, :], seg(hb_)[:, :, 0], reads=["hb"], writes=[("fin", 1, n)])
                TT_("dve", hf_, hf_, hb_, ALU.add, reads=["hf", "hb", ("fin", 0, n), ("fin", 1, n)], writes=["hsum0"])
                TT_("dve", MX[:, 8 + n, :], hf_, MX[:, 8 + n, :], ALU.mult, reads=["hsum0"], writes=["hsum", ("mxl", n)])
            for d in range(2):
                DMA("sp", st_o[l, d].rearrange("n p s -> p n s"), fin[:, d, :, :], "d_so",
                    reads=[("fin", d, n) for n in range(4)])

            chk('B2', l)
            S.barrier()
            ar.reset(prefix_off)
            rstd2 = ar.f32(T)
            xin_ = [ar.f32(TT) for _ in range(4)]
            xo = [ar.f32(TT) for _ in range(4)]
            sq2 = [ar.bf16(TT) for _ in range(2)]
            for hf in range(2):
                ssb = [6, 7]
                steps = [(cb, t2) for cb in range(NCH) for t2 in range(2)]

                def cload(si):
                    cb, t2 = steps[si]
                    tt = hf * 2 + t2
                    k = si % 4
                    DMA("sp", xin_[k], x_in[cb * P:(cb + 1) * P, tt * TT:(tt + 1) * TT], "d_ld%d" % k,
                        reads=[("x", l, cb, tt)], writes=[("xin", k)])

                for si in range(3):
                    cload(si)
                slot = rkey = None
                pend_c = [None]
                for si, (cb, t2) in enumerate(steps):
                    if si + 3 < len(steps):
                        cload(si + 3)
                    if t2 == 0 and cb % 2 == 0:
                        slot, rkey = ring_next(l * NS_LAYER + NS_MOD + NS_IN + cb // 2)
                    tt = hf * 2 + t2
                    tsl = slice(tt * TT, (tt + 1) * TT)
                    b_ = next_bank(0, 6)
                    for kc in range(NCH):
                        MM(bank(b_), slot[:, (cb % 2) * 16 + kc, :], MX[:, kc, tsl], kc == 0, kc == NCH - 1,
                           reads=[rkey], writes=[("ps", b_)])
                    k = si % 4
                    STT(xo[k], bank(b_), mod(2)[:, cb:cb + 1], xin_[k], ALU.mult, ALU.add,
                        reads=[("ps", b_), ("xin", k), ("mods", l)], writes=[("xo", k)])
                    DMA("act", xsB[cb * P:(cb + 1) * P, tsl], xo[k], "d_st%d" % k, reads=[("xo", k)], writes=[("xB", cb, tt)])
                    ACT(sq2[si % 2], xo[k], AF.Square, reads=[("xo", k)], writes=[("sq2", si % 2)])
                    if pend_c[0] is not None:
                        pend_c[0]()
                    pend_c[0] = (lambda si=si, t2=t2, cb=cb: MM(bank(ssb[t2]), ones_b, sq2[si % 2], cb == 0, cb == NCH - 1,
                                                               reads=[("sq2", si % 2), "cm"], writes=[("ps", ssb[t2])]))
                pend_c[0]()
                pend_c[0] = None
                for t2 in range(2):
                    tt = hf * 2 + t2
                    tsl = slice(tt * TT, (tt + 1) * TT)
                    ACT(rstd2[:, tsl], bank(ssb[t2]), AF.Sqrt, reads=[("ps", ssb[t2])], writes=[("r2a", tt)],
                        bias=EPS, scale=1.0 / D)
                    S.op("dve", lambda e, tsl=tsl: e.reciprocal(rstd2[:, tsl], rstd2[:, tsl]), [("r2a", tt)], [("r2", tt)])

            chk('C', l)
            S.barrier()
            ar.reset(prefix_off + T)
            HT = 2 * TT
            stg2 = [ar.f32(TT) for _ in range(4)]
            tmp2 = [ar.f32(TT) for _ in range(2)]
            sg = [ar.f32(TT) for _ in range(2)]
            xo2 = [ar.f32(TT) for _ in range(4)]
            sq3 = [ar.bf16(TT) for _ in range(2)]
            stg3 = [ar.f32(TT) for _ in range(4)]
            end_small = ar.off
            ar.reset(0)
            h2 = ar.bf16(NCH, HT)
            actb_a = None
            if ar.off + (NFF * HT) // 2 <= prefix_off:
                actb = ar.bf16(NFF, HT)
            else:
                n_pre = (prefix_off - ar.off) * 2 // HT
                act_pre = ar.bf16(n_pre, HT)
                ar.reset(end_small)
                act_post = ar.bf16(NFF - n_pre, HT)
                actb = None
            if actb is None:
                def act_at(j):
                    return act_pre[:, j, :] if j < n_pre else act_post[:, j - n_pre, :]
            else:
                def act_at(j):
                    return actb[:, j, :]
            last = (l == nl - 1)
            x_out = xsA
            modgen = mod_steps(l + 1) if not last else None

            def mod_step():
                nonlocal modgen
                if modgen is not None:
                    try:
                        next(modgen)
                    except StopIteration:
                        modgen = None
            def h2_steps(hf):
                steps = [(c, t2) for t2 in range(2) for c in range(NCH)]

                def dload(si):
                    c, t2 = steps[si]
                    tt = hf * 2 + t2
                    k = si % 4
                    DMA("sp", stg3[k], xsB[c * P:(c + 1) * P, tt * TT:(tt + 1) * TT], "d_h%d" % k,
                        reads=[("xB", c, tt)], writes=[("stg3", k)])

                for si in range(3):
                    dload(si)
                for si, (c, t2) in enumerate(steps):
                    if si + 3 < len(steps):
                        dload(si + 3)
                    tt = hf * 2 + t2
                    k = si % 4
                    STT(tmp2[si % 2], stg3[k], gs2[:, c:c + 1], rstd2[:, tt * TT:(tt + 1) * TT], ALU.mult, ALU.mult,
                        reads=[("stg3", k), "gs2", ("r2", tt)], writes=[("tmp2", si % 2)])
                    ACT(h2[:, c, t2 * TT:(t2 + 1) * TT], tmp2[si % 2], AF.Identity,
                        reads=[("tmp2", si % 2), ("mods", l)], writes=[("h2", c, t2)], bias=mod(3)[:, c:c + 1])
                    yield si

            h2gen1 = h2_steps(1)
            for hf in range(2):
                if hf == 0:
                    for _ in h2_steps(0):
                        pass
                else:
                    for _ in h2gen1:
                        pass
                for j in range(NFF):
                    slot, rkey = ring_next(l * NS_LAYER + NS_MOD + NS_IN + NS_OUT + j)
                    for t2 in range(2):
                        bg = next_bank(0, 6)
                        bu = next_bank(0, 6)
                        hr = [("h2", c, t2) for c in range(NCH)]
                        for kc in range(NCH):
                            MM(bank(bg), slot[:, kc, :], h2[:, kc, t2 * TT:(t2 + 1) * TT], kc == 0, kc == NCH - 1,
                               reads=[rkey] + (hr if kc == 0 else []), writes=[("ps", bg)])
                        for kc in range(NCH):
                            MM(bank(bu), slot[:, 16 + kc, :], h2[:, kc, t2 * TT:(t2 + 1) * TT], kc == 0, kc == NCH - 1,
                               reads=[rkey], writes=[("ps", bu)])
                        k2 = (j * 2 + t2) % 2
                        ACT(sg[k2], bank(bg), AF.Silu, reads=[("ps", bg)], writes=[("sg", k2)])
                        TT_("dve", act_at(j)[:, t2 * TT:(t2 + 1) * TT], bank(bu), sg[k2], ALU.mult,
                            reads=[("ps", bu), ("sg", k2)], writes=[("act", j, t2)])
                    if j % 2 == 1 or j in (0, 10, 20):
                        mod_step()
                steps = [(m, t2) for m in range(NCH) for t2 in range(2)]

                def eload(si):
                    m, t2 = steps[si]
                    tt = hf * 2 + t2
                    k = si % 4
                    DMA("sp", stg2[k], xsB[m * P:(m + 1) * P, tt * TT:(tt + 1) * TT], "d_ld%d" % k,
                        reads=[("xB", m, tt)], writes=[("stg2", k)])

                for si in range(3):
                    eload(si)
                ssb = [6, 7]
                cur_s = {"idx": -1, "slot": None, "rkey": None}
                pend_d = []
                for m in range(NCH + 1):
                    if m == NCH:
                        for f_ in pend_d:
                            f_()
                        pend_d = []
                        break
                    bks = [next_bank(0, 6), next_bank(0, 6)]
                    for kc in range(NFF):
                        if kc == 8 and pend_d:
                            for f_ in pend_d:
                                f_()
                            pend_d = []
                        bidx = m * NFF + kc
                        sidx = bidx // 32
                        if sidx != cur_s["idx"]:
                            cur_s["slot"], cur_s["rkey"] = ring_next(l * NS_LAYER + NS_MOD + NS_IN + NS_OUT + NS_W1 + sidx)
                            cur_s["idx"] = sidx
                        slot, rkey = cur_s["slot"], cur_s["rkey"]
                        for t2 in range(2):
                            MM(bank(bks[t2]), slot[:, bidx % 32, :], act_at(kc)[:, t2 * TT:(t2 + 1) * TT],
                               kc == 0, kc == NFF - 1, reads=[rkey, ("act", kc, t2)], writes=[("ps", bks[t2])])
                    for t2 in range(2):
                        si = m * 2 + t2
                        if si + 3 < len(steps):
                            eload(si + 3)
                        tt = hf * 2 + t2
                        tsl = slice(tt * TT, (tt + 1) * TT)
                        b_ = bks[t2]
                        k = si % 4
                        STT(xo2[k], bank(b_), mod(5)[:, m:m + 1], stg2[k], ALU.mult, ALU.add,
                            reads=[("ps", b_), ("stg2", k), ("mods", l)], writes=[("xo2", k)])
                        DMA("act", x_out[m * P:(m + 1) * P, tsl], xo2[k], "d_st%d" % k, reads=[("xo2", k)],
                            writes=[("x", l + 1, m, tt)])
                        if last:
                            ACT(sq3[si % 2], xo2[k], AF.Square, reads=[("xo2", k)], writes=[("sq3", si % 2)])
                            pend_d.append(lambda si=si, t2=t2, m=m: MM(bank(ssb[t2]), ones_b, sq3[si % 2], m == 0, m == NCH - 1,
                                                                       reads=[("sq3", si % 2), "cm"], writes=[("ps", ssb[t2])]))
                    if hf == 0:
                        for _ in range(2):
                            next(h2gen1, None)
                if last:
                    for t2 in range(2):
                        tt = hf * 2 + t2
                        tsl = slice(tt * TT, (tt + 1) * TT)
                        ACT(rstd2[:, tsl], bank(ssb[t2]), AF.Sqrt, reads=[("ps", ssb[t2])], writes=[("r3a", tt)],
                            bias=EPS, scale=1.0 / D)
                        S.op("dve", lambda e, tsl=tsl: e.reciprocal(rstd2[:, tsl], rstd2[:, tsl]), [("r3a", tt)], [("r3", tt)])
                    steps = [(c, t2) for t2 in range(2) for c in range(NCH)]

                    def fload(si):
                        c, t2 = steps[si]
                        tt = hf * 2 + t2
                        k = si % 4
                        DMA("sp", stg2[k], xsA[c * P:(c + 1) * P, tt * TT:(tt + 1) * TT], "d_ld%d" % k,
                            reads=[("x", l + 1, c, tt)], writes=[("stg2", k)])

                    for si in range(3):
                        fload(si)
                    for si, (c, t2) in enumerate(steps):
                        if si + 3 < len(steps):
                            fload(si + 3)
                        tt = hf * 2 + t2
                        tsl = slice(tt * TT, (tt + 1) * TT)
                        k = si % 4
                        STT(xo2[k], stg2[k], vecs[:, V_NFIN + c:V_NFIN + c + 1], rstd2[:, tsl], ALU.mult, ALU.mult,
                            reads=[("stg2", k), "vecs", ("r3", tt)], writes=[("xo2", k)])
                        DMA("sp", yT[c * P:(c + 1) * P, tsl], xo2[k], "d_st%d" % k, reads=[("xo2", k)])
            while modgen is not None:
                mod_step()

        def chk(tag, l=0):
            if STOP_AT == tag or STOP_AT == "%s%d" % (tag, l):
                raise _Stop()

        try:
            chk("pro")
            for l in range(nl):
                layer(l)
        except _Stop:
            pass
        S.barrier()
        final_waits = [(k, v) for k, v in S.dval.items()]
        S.stream["sp"].append((final_waits, None, None))

        def replay(eng_name):
            def run(e):
                for waits, fn, dsem in S.stream[eng_name]:
                    for key, val in waits:
                        e.wait_ge(semh[key], val)
                    if fn is None:
                        continue
                    ins = fn(e)
                    if dsem is not None:
                        ins.then_inc(semh[dsem], 16)
                    else:
                        ins.then_inc(semh[eng_name], 1)
            return run

        block.tensor(replay("pe"))
        block.vector(replay("dve"))
        block.scalar(replay("act"))
        block.gpsimd(replay("pool"))
        block.sync(replay("sp"))
    return nc, ring_state["rec"]


def _fm(v):
    v = np.asarray(v, np.float32).reshape(-1, P)
    return np.ascontiguousarray(v.T)


def _tile_w(w):
    K, N = w.shape
    kc, cb = K // P, N // P
    b = w.reshape(kc, P, cb, P).transpose(2, 0, 1, 3)
    b = b.reshape(cb * kc, P, P)
    nb = b.shape[0]
    assert nb % 32 == 0
    return np.ascontiguousarray(b.reshape(nb // 32, 32, P, P).transpose(0, 2, 1, 3).reshape(nb // 32, P, 32 * P))


def _tile_w1(w1):
    K, N = w1.shape
    ga = w1[:, :N // 2].reshape(NCH, P, NFF, P)
    up = w1[:, N // 2:].reshape(NCH, P, NFF, P)
    s = np.stack([ga, up], axis=0)
    s = s.transpose(3, 2, 0, 1, 4)
    return np.ascontiguousarray(s.reshape(NFF, P, 32 * P))


def _rope_tables(seg_len, rope):
    if not rope:
        return np.ones((P, T), np.float32), np.zeros((P, T), np.float32)
    t = np.arange(T)
    row = (t // 64).astype(np.float32)
    col = (t % 64).astype(np.float32)
    rd = HD // 4
    inv = (np.float32(10000.0) ** (-np.arange(rd, dtype=np.float32) / np.float32(rd))).astype(np.float32)
    ang = np.zeros((T, 2, 2, rd), np.float32)
    ang[:, 0, :, :] = (row[:, None] * inv[None, :])[:, None, :]
    ang[:, 1, :, :] = (col[:, None] * inv[None, :])[:, None, :]
    ang = ang.reshape(T, HD)
    return np.ascontiguousarray(np.cos(ang).T.astype(np.float32)), np.ascontiguousarray(np.sin(ang).T.astype(np.float32))


def _band_bias(sample):
    NEG = np.float32(-1e30)
    b = np.zeros((P, NBK, 384), np.float32)
    i = np.arange(P)[:, None]
    j = np.arange(P)[None, :]
    for qb in range(NBK):
        if sample:
            prev = np.where(j >= i, 0.0, NEG) if qb >= 1 else np.full((P, P), NEG)
            nxt = np.where(j <= i, 0.0, NEG) if qb <= NBK - 2 else np.full((P, P), NEG)
        else:
            prev = np.zeros((P, P)) if qb % 2 == 1 else np.full((P, P), NEG)
            nxt = np.zeros((P, P)) if qb % 2 == 0 else np.full((P, P), NEG)
        b[:, qb, 0:128] = prev
        b[:, qb, 256:384] = nxt
    return b.reshape(P, NBK * 384)


def _pool_tables(seg_len):
    wins = (2, 4, 8, 16)
    pint = np.zeros((P, 4, 112), np.float32)
    pedge = np.zeros((P, NBK, 4, 4, 8), np.float32)
    for g, w in enumerate(wins):
        M = np.zeros((T, T), np.float32)
        for s0 in range(0, T, seg_len):
            for tl in range(seg_len):
                lo = min(max(tl - w // 2, 0), seg_len)
                hi = min(max(tl + w // 2, 0), seg_len)
                to = s0 + tl
                M[s0 + lo:s0 + hi, to] = np.float32(1.0) / np.float32(hi - lo)
                M[to, to] -= 1.0
        blk = M[128:256, 128:256] if seg_len >= 384 else None
        for bi in range(NBK):
            d = M[bi * P:(bi + 1) * P, bi * P:(bi + 1) * P]
            if bi == 0:
                pint[:, g, :] = d[:, 8:120]
            else:
                assert np.array_equal(pint[:, g, :], d[:, 8:120])
            pedge[:, bi, g, 1, :] = d[:, 0:8]
            pedge[:, bi, g, 2, :] = d[:, 120:128]
            if bi > 0:
                pedge[:, bi, g, 0, :] = M[(bi - 1) * P:bi * P, bi * P:bi * P + 8]
            if bi < NBK - 1:
                pedge[:, bi, g, 3, :] = M[(bi + 1) * P:(bi + 2) * P, bi * P + 120:(bi + 1) * P]
    return pint.reshape(P, 4 * 112), pedge.reshape(P, NBK * 4 * 4 * 8)


def _cmat():
    ones = np.ones((P, P), np.float32)
    R = np.zeros((P, P), np.float32)
    for m in range(P):
        if (m // 32) % 2 == 0:
            R[m + 32, m] = -1.0
        else:
            R[m - 32, m] = 1.0
    I = np.eye(P, dtype=np.float32)
    return np.ascontiguousarray(np.concatenate([ones, R, I], axis=1))


def prepare_inputs(inp, nl=DEPTH):
    f = lambda k: np.asarray(inp[k], np.float32)
    x_prompt, x_sample = f("x_prompt"), f("x_sample")
    cache_k, cache_v, state_lru = f("cache_k"), f("cache_v"), f("state_lru")
    c, c_ctx = f("c"), f("c_ctx")
    slots = []
    for l in range(nl):
        slots.append(_tile_w(f("mod_w")[l]))
        slots.append(_tile_w(f("w_in")[l]))
        slots.append(_tile_w(f("w_out")[l]))
        slots.append(_tile_w1(f("ffn_w1")[l]))
        slots.append(_tile_w(f("ffn_w2")[l]))
    wall = np.concatenate(slots, axis=0)
    assert wall.shape[0] == nl * NS_LAYER, wall.shape
    lruw = np.zeros((DEPTH, P, 16, P), np.float32)
    wa, wx = f("lru_wa"), f("lru_wx")
    for l in range(DEPTH):
        for d in range(2):
            for n in range(4):
                lruw[l, :, d * 4 + n, :] = wa[l, d, n]
                lruw[l, :, 8 + d * 4 + n, :] = wx[l, d, n]
    lruw = lruw.reshape(DEPTH, P, 16 * P)
    poolw = np.ascontiguousarray(f("pool_w").transpose(0, 2, 1, 3)).reshape(DEPTH, P, 4 * P)
    cmat = _cmat()
    tabs = {}
    for kind, seg_len in (("p", 256), ("s", 2048)):
        cosT, sinT = _rope_tables(seg_len, kind == "s")
        pint, pedge = _pool_tables(seg_len)
        tabs[kind] = dict(cosT=cosT, sinT=sinT, biasb=_band_bias(kind == "s"), pint=pint, pedge=pedge)

    def vec_common():
        v = np.zeros((P, NV), np.float32)
        for l in range(DEPTH):
            b = l * V_LAYER
            v[:, b + V_MODB:b + V_MODB + 96] = _fm(f("mod_b")[l])
            v[:, b + V_NMIX:b + V_NMIX + 16] = _fm(f("norm_mix")[l])
            v[:, b + V_NFFN:b + V_NFFN + 16] = _fm(f("norm_ffn")[l])
            v[:, b + V_CONVW:b + V_CONVW + 16] = _fm(f("conv_w")[l])
            v[:, b + V_CONVB:b + V_CONVB + 4] = _fm(f("conv_b")[l])
            v[:, b + V_BA:b + V_BA + 8] = _fm(f("lru_ba")[l])
            v[:, b + V_BX:b + V_BX + 8] = _fm(f("lru_bx")[l])
            v[:, b + V_LAM:b + V_LAM + 8] = _fm(f("lru_lambda")[l])
            v[:, b + V_PSC:b + V_PSC + 4] = _fm(f("pool_scale")[l])
            v[:, b + V_SINK:b + V_SINK + 8] = f("attn_sink")[l][None, :]
        v[:, V_NFIN:V_NFIN + 16] = _fm(f("norm_final"))
        return v

    vbase = vec_common()
    zeros_ck = np.zeros((DEPTH, P, NKV * 512), np.float32)
    zeros_cv = np.zeros((DEPTH, 512, NKV * HD), np.float32)
    in_maps = []
    for ci in range(8):
        if ci in (4, 5):
            b = ci - 4
            kind = "s"
            x = x_sample[b]
            cond = c[b]
            v = vbase.copy()
            for l in range(DEPTH):
                v[:, l * V_LAYER + V_H0:l * V_LAYER + V_H0 + 8] = _fm(state_lru[b, l])
            v[:, V_FLAG] = 1.0
            v[:, V_CTXB] = 0.0
            ckT = np.ascontiguousarray(cache_k[b].transpose(0, 3, 2, 1)).reshape(DEPTH, P, NKV * 512)
            cv = np.ascontiguousarray(cache_v[b]).reshape(DEPTH, 512, NKV * HD)
        else:
            pc = ci if ci < 4 else ci - 6
            kind = "p"
            x = x_prompt[pc * 8:(pc + 1) * 8].reshape(T, D)
            cond = c_ctx
            v = vbase.copy()
            v[:, V_FLAG] = 0.0
            v[:, V_CTXB] = -1e30
            ckT, cv = zeros_ck, zeros_cv
        m = dict(xT=np.ascontiguousarray(x.T), cond=_fm(cond), vecs=v, wall=wall, lruw=lruw, poolw=poolw,
                 ckT=ckT, cv=cv, cmat=cmat)
        m.update(tabs[kind])
        in_maps.append(m)
    return in_maps


_PROG = {}


def kernel(**inputs):
    return run_step(inputs, DEPTH)


def run_step(inputs, nl, trace=False):
    in_maps = prepare_inputs(inputs, nl)
    if nl not in _PROG:
        _PROG[nl] = build_program(nl)
    nc = _PROG[nl]
    res = run_bass_kernel_spmd(nc, in_maps, core_ids=list(range(8)), **({'trace': True} if trace else {}))
    _PROG['last'] = res
    r = res.results
    B, SEQ = 32, 256
    y_prompt = np.zeros((B, SEQ, D), np.float32)
    new_k = np.zeros((B, DEPTH, SEQ, NKV, HD), np.float32)
    new_v = np.zeros((B, DEPTH, SEQ, NKV, HD), np.float32)
    new_s = np.zeros((B, DEPTH, 2, 512), np.float32)
    y_sample = np.zeros((2, T, D), np.float32)
    for ci in range(4):
        o = r[ci]
        sl = slice(ci * 8, (ci + 1) * 8)
        y_prompt[sl] = o["yT"].T.reshape(8, SEQ, D)
        new_k[sl] = o["kT_o"].reshape(DEPTH, NKV, HD, 8, SEQ).transpose(3, 0, 4, 1, 2)
        new_v[sl] = o["v_o"].reshape(DEPTH, 8, SEQ, NKV, HD).transpose(1, 0, 2, 3, 4)
        new_s[sl] = o["st_o"].transpose(4, 0, 1, 2, 3).reshape(8, DEPTH, 2, 512)
    for b in range(2):
        y_sample[b] = r[4 + b]["yT"].T
    return (y_prompt, y_sample, new_k, new_v, new_s)
```

```python
import numpy as np
import concourse.bass as bass
import concourse.mybir as mybir
from concourse.bass_utils import run_bass_kernel_spmd

F32 = mybir.dt.float32
BF16 = mybir.dt.bfloat16
AF = mybir.ActivationFunctionType
ALU = mybir.AluOpType
AX = mybir.AxisListType

P = 128
D = 2048
NCH = 16
T = 2048
TT = 512
NTT = 4
NBK = 16
DEPTH = 4
HD = 128
NH = 8
NKV = 2
NFF = 44
SCALE = float(HD) ** -0.5
EPS = 1e-6
NSLOT = 4
SLOT_COLS = 4096
NS_MOD, NS_IN, NS_OUT, NS_W1, NS_W2 = 48, 12, 8, 44, 22
NS_LAYER = NS_MOD + NS_IN + NS_OUT + NS_W1 + NS_W2
V_MODB, V_NMIX, V_NFFN, V_CONVW, V_CONVB, V_BA, V_BX, V_LAM, V_PSC, V_H0, V_SINK = (
    0, 96, 112, 128, 144, 148, 156, 164, 172, 176, 184)
V_LAYER = 192
V_NFIN = DEPTH * V_LAYER
V_FLAG = V_NFIN + 16
V_CTXB = V_FLAG + 1
NV = V_CTXB + 1

ENGS = ("pe", "dve", "act", "pool", "sp")


class Sched:
    def __init__(self):
        self.stream = {e: [] for e in ENGS}
        self.cnt = {e: 0 for e in ENGS}
        self.waited = {e: {} for e in ENGS}
        self.res = {}
        self.dval = {}

    def _need(self, eng, deps):
        wd = self.waited[eng]
        out = []
        for key, val in deps:
            if eng == "pe" and key == "pe":
                continue
            if wd.get(key, 0) >= val:
                continue
            wd[key] = val
            out.append((key, val))
        return out

    def op(self, eng, fn, reads=(), writes=(), dsem=None):
        deps = []
        for r in reads:
            st = self.res.get(r)
            if st is not None:
                if st[0] is not None:
                    deps.append(st[0])
                if isinstance(r, tuple) and r[0] == "ps":
                    deps.extend((k, v) for k, v in st[1].items() if k != eng)
        for w in writes:
            st = self.res.get(w)
            if st is not None:
                if st[0] is not None:
                    deps.append(st[0])
                deps.extend(st[1].items())
        if dsem is not None:
            prev = self.dval.get(dsem, 0)
            if prev:
                deps.append((dsem, prev))
            val = prev + 16
            self.dval[dsem] = val
            ev = (dsem, val)
        else:
            self.cnt[eng] += 1
            ev = (eng, self.cnt[eng])
        waits = self._need(eng, deps)
        self.stream[eng].append((waits, fn, dsem))
        for r in reads:
            st = self.res.get(r)
            if st is None:
                self.res[r] = [None, {ev[0]: ev[1]}]
            else:
                if st[1].get(ev[0], 0) < ev[1]:
                    st[1][ev[0]] = ev[1]
        for w in writes:
            self.res[w] = [ev, {}]
        return ev

    def barrier(self):
        deps = [(e, self.cnt[e]) for e in ("pe", "dve", "act", "pool") if self.cnt[e]]
        deps += [(k, v) for k, v in self.dval.items() if not k.startswith("d_ring")]
        for e in ENGS:
            waits = self._need(e, deps)
            if waits:
                self.stream[e].append((waits, None, None))
        self.res = {k: v for k, v in self.res.items() if isinstance(k, tuple) and k[0] == "ring"}


class Arena:
    def __init__(self, ap, ncols):
        self.ap = ap
        self.ncols = ncols
        self.off = 0

    def reset(self, off=0):
        self.off = off

    def _take(self, n4):
        assert self.off + n4 <= self.ncols, ("arena overflow", self.off, n4, self.ncols)
        v = self.ap[:, self.off:self.off + n4]
        self.off += n4
        return v

    @staticmethod
    def _shape(v, shape):
        if len(shape) == 1:
            return v
        if len(shape) == 2:
            return v.rearrange("p (a b) -> p a b", a=shape[0])
        if len(shape) == 3:
            return v.rearrange("p (a b c) -> p a b c", a=shape[0], b=shape[1])
        raise ValueError(shape)

    def f32(self, *shape):
        n = int(np.prod(shape))
        return self._shape(self._take(n), shape)

    def bf16(self, *shape):
        n = int(np.prod(shape))
        assert n % 2 == 0
        return self._shape(self._take(n // 2).bitcast(BF16), shape)


class _Stop(Exception):
    pass


STOP_AT = None


def build_program(nl=DEPTH, debug=False):
    _, plan = _build(nl, None)
    nc, plan2 = _build(nl, plan)
    assert plan2 == plan
    return nc


def _build(nl, plan_in):
    nc = bass.Bass("TRN2", target_bir_lowering=False)
    dt = nc.dram_tensor
    xT = dt("xT", [D, T], F32, kind="ExternalInput").ap()
    cond_d = dt("cond", [P, NCH], F32, kind="ExternalInput").ap()
    vecs_d = dt("vecs", [P, NV], F32, kind="ExternalInput").ap()
    wall = dt("wall", [nl * NS_LAYER, P, SLOT_COLS], F32, kind="ExternalInput").ap()
    lruw_d = dt("lruw", [DEPTH, P, 16 * 128], F32, kind="ExternalInput").ap()
    poolw_d = dt("poolw", [DEPTH, P, 4 * 128], F32, kind="ExternalInput").ap()
    ckT_d = dt("ckT", [DEPTH, P, NKV * 512], F32, kind="ExternalInput").ap()
    cv_d = dt("cv", [DEPTH, 512, NKV * HD], F32, kind="ExternalInput").ap()
    cos_d = dt("cosT", [P, T], F32, kind="ExternalInput").ap()
    sin_d = dt("sinT", [P, T], F32, kind="ExternalInput").ap()
    bias_d = dt("biasb", [P, NBK * 384], F32, kind="ExternalInput").ap()
    pint_d = dt("pint", [P, 4 * 112], F32, kind="ExternalInput").ap()
    pedge_d = dt("pedge", [P, NBK * 4 * 4 * 8], F32, kind="ExternalInput").ap()
    cmat_d = dt("cmat", [P, 3 * 128], F32, kind="ExternalInput").ap()

    yT = dt("yT", [D, T], F32, kind="ExternalOutput").ap()
    kT_o = dt("kT_o", [DEPTH, NKV, P, T], F32, kind="ExternalOutput").ap()
    v_o = dt("v_o", [DEPTH, T, NKV * HD], F32, kind="ExternalOutput").ap()
    st_o = dt("st_o", [DEPTH, 2, 4, P, 8], F32, kind="ExternalOutput").ap()
    xsA = dt("xsA", [D, T], F32, kind="Internal").ap()
    xsB = dt("xsB", [D, T], F32, kind="Internal").ap()

    S = Sched()
    ARENA_COLS = 42752
    PERS_COLS = 1664
    RING_COLS = NSLOT * SLOT_COLS // 2

    sem_names = ["pe", "dve", "act", "pool"] + ["d_ring%d" % i for i in range(NSLOT)]
    dyn_sems = ["d_ld%d" % i for i in range(6)] + ["d_st%d" % i for i in range(4)] + [
        "d_c%d" % i for i in range(12)] + ["d_h%d" % i for i in range(4)] + ["d_vo0", "d_vo1", "d_ko0", "d_ko1", "d_so"]
    sem_names += dyn_sems

    import contextlib
    with contextlib.ExitStack() as es:
        arena_t = es.enter_context(nc.sbuf_tensor("arena", [P, ARENA_COLS], F32))
        pers_t = es.enter_context(nc.sbuf_tensor("pers", [P, PERS_COLS], F32))
        ring_t = es.enter_context(nc.sbuf_tensor("ring", [P, NSLOT * SLOT_COLS], BF16))
        ps_t = es.enter_context(nc.psum_tensor("ps", [P, 8 * 512], F32))
        semh = {n: es.enter_context(nc.semaphore(n)) for n in sem_names}
        block = es.enter_context(nc.Block())

        ar = Arena(arena_t[:, :], ARENA_COLS)
        pr = Arena(pers_t[:, :], PERS_COLS)
        ring = [ring_t[:, i * SLOT_COLS:(i + 1) * SLOT_COLS] for i in range(NSLOT)]

        def bank(b):
            return ps_t[:, b * 512:(b + 1) * 512]

        def bank_bf(b):
            return ps_t[:, b * 512:(b + 1) * 512].bitcast(BF16)

        vecs = pr.f32(NV)
        cond = pr.f32(NCH)
        scb = pr.bf16(NCH)
        mods = pr.f32(DEPTH, 96)
        cm = pr.bf16(3, 128)
        ones_b, rot_b, ident_b = cm[:, 0, :], cm[:, 1, :], cm[:, 2, :]
        gs1 = pr.f32(NCH)
        gs2 = pr.f32(NCH)
        lam8 = pr.f32(8)
        cwe = pr.f32(16)
        sm = pr.f32(64)
        rstdF = None

        def DMA(q, out, in_, dsem, reads=(), writes=()):
            return S.op(q, lambda e: e.dma_start(out=out, in_=in_), reads, writes, dsem=dsem)

        def MM(out, lhsT, rhs, start, stop, reads=(), writes=()):
            for r_ in reads:
                if isinstance(r_, tuple) and r_[0] == "ring":
                    assert r_[1] == (ring_state["next"] - 1) % NSLOT, "stale ring slot"
            return S.op("pe", lambda e: e.matmul(out, lhsT, rhs, start=start, stop=stop), reads, writes)

        def TR(out, in_, reads=(), writes=()):
            return S.op("pe", lambda e: e.transpose(out, in_, ident_b), reads, writes)

        def ACT(out, in_, func, reads=(), writes=(), bias=None, scale=None):
            kw = {}
            if bias is not None:
                kw["bias"] = bias
            if scale is not None:
                kw["scale"] = scale
            return S.op("act", lambda e: e.activation(out, in_, func, **kw), reads, writes)

        def TT_(eng, out, in0, in1, op, reads=(), writes=()):
            return S.op(eng, lambda e: e.tensor_tensor(out, in0, in1, op), reads, writes)

        def TS(eng, out, in0, s1, s2, op0, op1=None, reads=(), writes=()):
            if op1 is None:
                return S.op(eng, lambda e: e.tensor_scalar(out, in0, s1, None, op0), reads, writes)
            return S.op(eng, lambda e: e.tensor_scalar(out, in0, s1, s2, op0, op1), reads, writes)

        def STT(out, in0, scalar, in1, op0, op1, reads=(), writes=()):
            return S.op("dve", lambda e: e.scalar_tensor_tensor(out, in0, scalar, in1, op0, op1), reads, writes)

        def CP(eng, out, in_, reads=(), writes=()):
            if eng == "act":
                return S.op("act", lambda e: e.copy(out, in_), reads, writes)
            return S.op(eng, lambda e: e.tensor_copy(out, in_), reads, writes)

        def MEMSET(eng, ap, val, writes=()):
            return S.op(eng, lambda e: e.memset(ap, val), (), writes)

        ring_state = {"issued": 0, "next": 0, "plan": list(plan_in) if plan_in is not None else [], "rec": []}

        def ring_issue_upto(k):
            plan = ring_state["plan"]
            while ring_state["issued"] < min(k, len(plan)):
                f = ring_state["issued"]
                slot = f % NSLOT
                src = wall[plan[f]].rearrange("p (a b) -> p a b", b=2048)
                dst = ring[slot].rearrange("p (a b) -> p a b", b=2048)
                DMA("pool", dst, src, "d_ring%d" % slot, writes=[("ring", slot)])
                ring_state["issued"] += 1

        def ring_next(idx):
            f = ring_state["next"]
            ring_state["next"] += 1
            ring_state["rec"].append(idx)
            if plan_in is None:
                ring_state["plan"].append(idx)
            else:
                assert ring_state["plan"][f] == idx
            ring_issue_upto(f + NSLOT)
            slot = f % NSLOT
            return ring[slot].rearrange("p (a b) -> p a b", b=128), ("ring", slot)

        pb = {"i": 0}

        def next_bank(lo=0, hi=8):
            b = lo + (pb["i"] % (hi - lo))
            pb["i"] += 1
            return b

        ld_rr = {"i": 0}

        DMA("sp", vecs, vecs_d, "d_c0", writes=["vecs"])
        DMA("sp", cond, cond_d, "d_c1", writes=["cond"])
        DMA("pool", cm.rearrange("p a b -> p (a b)"), cmat_d, "d_c2", writes=["cm"])
        ACT(scb, cond, AF.Silu, reads=["cond"], writes=["scb"])
        def mod_steps(l):
            mb = 7
            for nb in range(96):
                if nb % 2 == 0:
                    slot, rkey = ring_next(l * NS_LAYER + nb // 2)
                for kc in range(NCH):
                    MM(bank(mb)[:, nb:nb + 1], slot[:, (nb % 2) * 16 + kc, :], scb[:, kc:kc + 1],
                       kc == 0, kc == NCH - 1, reads=[rkey, "scb", "cm"], writes=[("ps", mb)])
                if nb % 2 == 1:
                    yield nb
            TT_("dve", mods[:, l, :], bank(mb)[:, 0:96], vecs[:, l * V_LAYER + V_MODB:l * V_LAYER + V_MODB + 96],
                ALU.add, reads=[("ps", mb), "vecs"], writes=[("mods", l)])

        for _ in mod_steps(0):
            pass

        PREFIX = 0

        def layer(l):
            vb = l * V_LAYER
            x_in = xT if l == 0 else xsA

            def vcol(off, n=1):
                return vecs[:, vb + off:vb + off + n]

            def mod(j):
                return mods[:, l, j * 16:(j + 1) * 16]

            STT(gs1, mod(1), 1.0, vcol(V_NMIX, 16), ALU.add, ALU.mult, reads=[("mods", l), "vecs"], writes=["gs1"])
            STT(gs2, mod(4), 1.0, vcol(V_NFFN, 16), ALU.add, ALU.mult, reads=[("mods", l), "vecs"], writes=["gs2"])
            ACT(lam8, vcol(V_LAM, 8), AF.Sigmoid, reads=["vecs"], writes=["lam8a"])
            ACT(lam8, lam8, AF.Ln, reads=["lam8a"], writes=["lam8b"])
            TS("dve", lam8, lam8, 8.0, None, ALU.mult, reads=["lam8b"], writes=["lam8"])
            TS("dve", cwe, vcol(V_CONVW, 16), vecs[:, V_FLAG:V_FLAG + 1], None, ALU.mult, reads=["vecs"], writes=["cwe"])

            S.barrier()
            ar.reset(0)
            MX = ar.bf16(16, T)
            kT = ar.bf16(NKV, 18 * 128)
            vS = ar.bf16(18, NKV, 132)
            lx = ar.bf16(4, T)
            prefix_off = ar.off
            cosT = ar.bf16(T)
            sinT = ar.bf16(T)
            hTs = [ar.bf16(NCH, TT) for _ in range(2)]
            stg = [ar.f32(TT) for _ in range(4)]
            sq = [ar.bf16(TT) for _ in range(2)]
            tmpn = [ar.f32(TT) for _ in range(2)]
            rstd = ar.f32(TT)
            qb16 = [ar.bf16(TT) for _ in range(2)]
            qc = [ar.f32(TT) for _ in range(2)]
            qs1 = ar.f32(TT)
            qs = [qs1, qs1]
            kf1 = ar.f32(TT)
            kf = [kf1, kf1]
            vf = [ar.f32(256) for _ in range(2)]

            DMA("pool", cosT, cos_d, "d_c3", writes=["cos"])
            DMA("pool", sinT, sin_d, "d_c4", writes=["sin"])
            MEMSET("pool", kT[:, :, 0:128], 0.0, writes=[("kT", 0, -1), ("kT", 1, -1)])
            MEMSET("pool", kT[:, :, 17 * 128:18 * 128], 0.0, writes=[("kT", 0, 16), ("kT", 1, 16)])
            MEMSET("pool", vS[:, 0, :, :], 0.0, writes=[("vS", -1)])
            MEMSET("pool", vS[:, 17, :, :], 0.0, writes=[("vS", 16)])
            S.op("pool", lambda e: e.memset(vS[:, :, :, 128:129], 1.0), (), ["vones"] + [("vS", b_) for b_ in range(-1, 17)])

            def xload(c, tt, k):
                return DMA("sp", stg[k], x_in[c * P:(c + 1) * P, tt * TT:(tt + 1) * TT], "d_ld%d" % k,
                           reads=[("x", l, c, tt)], writes=[("stg", k)])

            SSB = 7

            def norm_steps(tt, hbuf, hk):
                for c in range(min(3, NCH)):
                    xload(c, tt, c % 4)
                for c in range(NCH):
                    if c + 3 < NCH:
                        xload(c + 3, tt, (c + 3) % 4)
                    ACT(sq[c % 2], stg[c % 4], AF.Square, reads=[("stg", c % 4)], writes=[("sq", c % 2)])
                    MM(bank(SSB), ones_b, sq[c % 2], c == 0, c == NCH - 1,
                       reads=[("sq", c % 2), "cm"], writes=[("ps", SSB)])
                    yield c
                ACT(rstd, bank(SSB), AF.Sqrt, reads=[("ps", SSB)], writes=["rstd_a"], bias=EPS, scale=1.0 / D)
                S.op("dve", lambda e: e.reciprocal(rstd, rstd), ["rstd_a"], ["rstd"])
                yield -1
                for c in range(min(3, NCH)):
                    xload(c, tt, c % 4)
                for c in range(NCH):
                    if c + 3 < NCH:
                        xload(c + 3, tt, (c + 3) % 4)
                    STT(tmpn[c % 2], stg[c % 4], gs1[:, c:c + 1], rstd, ALU.mult, ALU.mult,
                        reads=[("stg", c % 4), "gs1", "rstd"], writes=[("tmpn", c % 2)])
                    ACT(hbuf[:, c, :], tmpn[c % 2], AF.Identity, reads=[("tmpn", c % 2), ("mods", l)],
                        writes=[("hT", hk, c)], bias=mod(0)[:, c:c + 1])
                    yield c

            for _ in norm_steps(0, hTs[0], 0):
                pass
            for tt in range(NTT):
                tsl = slice(tt * TT, (tt + 1) * TT)
                hT = hTs[tt % 2]
                hk = tt % 2
                ngen = norm_steps(tt + 1, hTs[(tt + 1) % 2], (tt + 1) % 2) if tt + 1 < NTT else None

                def weave(n=2):
                    if ngen is not None:
                        for _ in range(n):
                            next(ngen, None)

                hreads = [("hT", hk, c) for c in range(NCH)]
                slot = rkey = None
                pend_a = [None]
                for cb in range(24):
                    if cb % 2 == 0:
                        slot, rkey = ring_next(l * NS_LAYER + NS_MOD + cb // 2)
                    if cb == 10:
                        for bi in range(4):
                            blk = tt * 4 + bi
                            b_ = next_bank(0, 7)
                            for kc in range(NCH):
                                rhs = slot[:, kc:kc + 17:16, :]
                                MM(bank(b_)[:, 0:256], hT[:, kc, bi * P:(bi + 1) * P], rhs, kc == 0, kc == NCH - 1,
                                   reads=[rkey] + (hreads if kc == 0 else []), writes=[("ps", b_)])
                            if pend_a[0] is not None:
                                pend_a[0]()
                                pend_a[0] = None
                            k2 = blk % 2
                            CP("act", vf[k2], bank(b_)[:, 0:256], reads=[("ps", b_)], writes=[("vf", k2)])
                            CP("dve", vS[:, blk + 1, :, 0:128], bank(b_)[:, 0:256].rearrange("p (a b) -> p a b", a=2),
                               reads=[("ps", b_)], writes=[("vS", blk)])
                            DMA("act", v_o[l, blk * P:(blk + 1) * P, :], vf[k2], "d_vo%d" % k2, reads=[("vf", k2)])
                            weave()
                        continue
                    if cb == 11:
                        continue
                    b_ = next_bank(0, 7)
                    for kc in range(NCH):
                        MM(bank(b_), slot[:, (cb % 2) * 16 + kc, :], hT[:, kc, :], kc == 0, kc == NCH - 1,
                           reads=[rkey] + (hreads if kc == 0 else []), writes=[("ps", b_)])
                    if pend_a[0] is not None:
                        pend_a[0]()
                        pend_a[0] = None
                    if cb < 10:
                        i2 = cb % 2
                        CP("act", qb16[i2], bank(b_), reads=[("ps", b_)], writes=[("qb16", i2)])
                        TT_("dve", qc[i2], bank(b_), cosT[:, tsl], ALU.mult, reads=[("ps", b_), "cos"], writes=[("qc", i2)])

                        def rope_tail(cb=cb, i2=i2, tt=tt, tsl=tsl):
                            b2 = next_bank(0, 7)
                            MM(bank(b2), rot_b, qb16[i2], True, True, reads=[("qb16", i2), "cm"], writes=[("ps", b2)])
                            TT_("dve", qs[i2], bank(b2), sinT[:, tsl], ALU.mult, reads=[("ps", b2), "sin"], writes=[("qs", 0)])
                            if cb < 8:
                                TT_("dve", MX[:, cb, tsl], qc[i2], qs[i2], ALU.add,
                                    reads=[("qc", i2), ("qs", 0)], writes=[("mx", cb, tt)])
                            else:
                                kv = cb - 8
                                TT_("dve", kf[kv], qc[i2], qs[i2], ALU.add,
                                    reads=[("qc", i2), ("qs", 0)], writes=[("kf", 0)])
                                CP("act", kT[:, kv, P + tt * TT:P + (tt + 1) * TT], kf[kv], reads=[("kf", 0)],
                                   writes=[("kT", kv, tt * 4 + j) for j in range(4)])
                                DMA("act", kT_o[l, kv, :, tsl], kf[kv], "d_ko0", reads=[("kf", 0)])

                        pend_a[0] = rope_tail
                    elif cb < 16:
                        CP("act", lx[:, cb - 12, tsl], bank(b_), reads=[("ps", b_)], writes=[("lx", cb - 12, tt)])
                    elif cb < 20:
                        ACT(MX[:, 8 + cb - 16, tsl], bank(b_), AF.Gelu_apprx_tanh, reads=[("ps", b_)],
                            writes=[("mx", 8 + cb - 16, tt)])
                    else:
                        CP("dve", MX[:, 12 + cb - 20, tsl], bank(b_), reads=[("ps", b_)], writes=[("mx", 12 + cb - 20, tt)])
                    weave()
                if ngen is not None:
                    for _ in ngen:
                        pass

            chk('A', l)
            S.barrier()
            ar.reset(prefix_off)
            biasb = ar.bf16(NBK, 384)
            ckT = ar.bf16(NKV, 512)
            cvS = ar.bf16(4, NKV, 132)
            pint = ar.bf16(4, 112)
            pedge = ar.bf16(NBK * 4 * 4, 8)
            poolw = ar.bf16(4, 128)
            zS = ar.bf16(NBK, 4 * 128)
            Sb = [ar.f32(384) for _ in range(2)]
            Pb = [ar.bf16(896) for _ in range(2)]
            PT = [ar.bf16(7, 128) for _ in range(2)]
            Osb = [ar.bf16(128) for _ in range(2)]
            DMA("pool", biasb.rearrange("p a b -> p (a b)").rearrange("p (a b) -> p a b", b=2048),
                bias_d.rearrange("p (a b) -> p a b", b=2048), "d_c5", writes=["biasb"])
            DMA("pool", ckT.rearrange("p a b -> p (a b)"), ckT_d[l], "d_c6", writes=["ckT"])
            for b4 in range(4):
                DMA("pool", cvS[:, b4, :, 0:128], cv_d[l, b4 * P:(b4 + 1) * P, :].rearrange("p (k d) -> p k d", k=NKV),
                    "d_c7", writes=[("cvSb", b4)])
            S.op("pool", lambda e: e.memset(cvS[:, :, :, 128:129], 1.0), (), ["cvones"])
            DMA("pool", pint.rearrange("p a b -> p (a b)"), pint_d, "d_c8", writes=["pint"])
            DMA("pool", pedge.rearrange("p a b -> p (a b)").rearrange("p (a b) -> p a b", b=2048),
                pedge_d.rearrange("p (a b) -> p a b", b=2048), "d_c9", writes=["pedge"])
            DMA("pool", poolw.rearrange("p a b -> p (a b)"), poolw_d[l], "d_c10", writes=["poolw"])

            items = [(qb, h) for qb in range(NBK) for h in range(NH)]
            nit = len(items)
            SP_ = [(0, 1), (2, 3)]
            PTB = 4
            OBS = [5, 6]
            OTB = 7
            st = vecs[:, vb + V_SINK:vb + V_SINK + 8]
            ctxb = vecs[:, V_CTXB:V_CTXB + 1]

            def smv(i, j):
                c0 = 8 * (i % 4) + j
                return sm[:, c0:c0 + 1]

            def phaseA(i):
                qb, h = items[i]
                kv = h // 4
                k2 = i % 2
                k4 = i % 4
                b0, b1 = SP_[k2]
                Sp = ps_t[:, b0 * 512:(b1 + 1) * 512]
                q_ap = MX[:, h, qb * P:(qb + 1) * P]
                MM(Sp[:, 128:512], q_ap, kT[:, kv, qb * P:(qb + 3) * P], True, True,
                   reads=[("mx", h, qb)] + [("kT", kv, j) for j in (qb - 1, qb, qb + 1)],
                   writes=[("ps", b0), ("ps", b1)])
                MM(Sp[:, 512:1024], q_ap, ckT[:, kv, :], True, True, reads=["ckT"], writes=[("ps", b1)])
                mx_, nb_, nb2, es_ = smv(i, 0), smv(i, 1), smv(i, 2), smv(i, 3)
                S.op("dve", lambda e: e.reduce_max(mx_, Sp[:, 128:1024], AX.X), [("ps", b0), ("ps", b1)], [("mx_", k4)])
                TS("dve", nb_, mx_, -SCALE, None, ALU.mult, reads=[("mx_", k4)], writes=[("nb", k4)])
                TS("dve", nb2, mx_, -SCALE, ctxb, ALU.mult, ALU.add, reads=[("mx_", k4), "vecs"], writes=[("nb2", k4)])
                TT_("dve", Sb[k2], Sp[:, 128:512], biasb[:, qb, :], ALU.add, reads=[("ps", b0), "biasb"], writes=[("Sb", k2)])
                ACT(Pb[k2][:, 0:384], Sb[k2], AF.Exp, reads=[("Sb", k2), ("nb", k4)], writes=[("Pb", k2)],
                    bias=nb_, scale=SCALE)
                ACT(Pb[k2][:, 384:896], Sp[:, 512:1024], AF.Exp, reads=[("ps", b1), ("nb2", k4)], writes=[("Pb", k2)],
                    bias=nb2, scale=SCALE)
                ACT(es_, st[:, h:h + 1], AF.Exp, reads=["vecs", ("nb", k4)], writes=[("es", k4)], bias=nb_, scale=1.0)

            def phaseT(i):
                k2 = i % 2
                ptp = bank_bf(PTB)
                for j in range(7):
                    TR(ptp[:, j * P:(j + 1) * P], Pb[k2][:, j * P:(j + 1) * P], reads=[("Pb", k2), "cm"], writes=[("ps", PTB)])
                CP("act" if i % 3 else "dve", PT[k2].rearrange("p a b -> p (a b)"), ptp[:, 0:896], reads=[("ps", PTB)], writes=[("PT", k2)])

            def phaseV(i):
                qb, h = items[i]
                kv = h // 4
                k2 = i % 2
                k4 = i % 4
                ob = OBS[k2]
                O = bank(ob)[:, 0:129]
                for j in range(7):
                    if j < 3:
                        rhs = vS[:, qb + j, kv, 0:129]
                        rd = [("vS", qb + j - 1), "vones"]
                    else:
                        rhs = cvS[:, j - 3, kv, 0:129]
                        rd = [("cvSb", j - 3), "cvones"]
                    MM(O, PT[k2][:, j, :], rhs, j == 0, j == 6, reads=[("PT", k2)] + rd, writes=[("ps", ob)])
                den, es_ = smv(i, 4), smv(i, 3)
                TT_("dve", den, bank(ob)[:, 128:129], es_, ALU.add, reads=[("ps", ob), ("es", k4)], writes=[("den", k4)])
                S.op("dve", lambda e: e.reciprocal(den, den), [("den", k4)], [("rden", k4)])
                ACT(Osb[k2], bank(ob)[:, 0:128], AF.Copy, reads=[("ps", ob), ("rden", k4)], writes=[("Osb", k2)], scale=den)

            def phaseC(i):
                qb, h = items[i]
                k2 = i % 2
                otp = bank_bf(OTB)
                TR(otp[:, 0:128], Osb[k2], reads=[("Osb", k2), "cm"], writes=[("ps", OTB)])
                CP("dve", MX[:, h, qb * P:(qb + 1) * P], otp[:, 0:128], reads=[("ps", OTB)], writes=[("mx", h, qb)])

            for i in range(nit + 3):
                if i < nit:
                    phaseA(i)
                if 0 <= i - 1 < nit:
                    phaseT(i - 1)
                if 0 <= i - 2 < nit:
                    phaseV(i - 2)
                if 0 <= i - 3 < nit:
                    phaseC(i - 3)

            chk('attn', l)
            pb["i"] = 0
            for bi in range(NBK):
                b_ = next_bank(0, 4)
                for g in range(4):
                    MM(bank(b_)[:, g * P:(g + 1) * P], MX[:, 12 + g, bi * P:(bi + 1) * P], poolw[:, g, :], True, True,
                       reads=["poolw", ("mx", 12 + g, bi)], writes=[("ps", b_)])
                CP("act" if bi % 2 else "dve", zS[:, bi, :], bank(b_), reads=[("ps", b_)], writes=[("zS", bi)])
            pe4 = pedge.rearrange("p (i g k) e -> p i g k e", i=NBK, g=4, k=4)
            for g in range(4):
                for gi in range(4):
                    b_ = next_bank(4, 8)
                    for j in range(4):
                        bi = gi * 4 + j
                        o = bank(b_)[:, j * P:(j + 1) * P]
                        zi = zS[:, bi, g * P:(g + 1) * P]
                        MM(o[:, 8:120], zi, pint[:, g, :], True, True, reads=[("zS", bi), "pint"], writes=[("ps", b_)])
                        zp = zS[:, max(bi - 1, 0), g * P:(g + 1) * P]
                        zn = zS[:, min(bi + 1, NBK - 1), g * P:(g + 1) * P]
                        MM(o[:, 0:8], zp, pe4[:, bi, g, 0, :], True, False,
                           reads=[("zS", max(bi - 1, 0)), "pedge"], writes=[("ps", b_)])
                        MM(o[:, 0:8], zi, pe4[:, bi, g, 1, :], False, True, writes=[("ps", b_)])
                        MM(o[:, 120:128], zi, pe4[:, bi, g, 2, :], True, False, writes=[("ps", b_)])
                        MM(o[:, 120:128], zn, pe4[:, bi, g, 3, :], False, True,
                           reads=[("zS", min(bi + 1, NBK - 1))], writes=[("ps", b_)])
                    TS("dve", MX[:, 12 + g, gi * TT:(gi + 1) * TT], bank(b_), vcol(V_PSC + g), None, ALU.mult,
                       reads=[("ps", b_), "vecs"], writes=[("mx", 12 + g, "o", gi)])

            chk('B1', l)
            S.barrier()
            ar.reset(prefix_off)
            lw = ar.bf16(16, 128)
            u = ar.f32(T)
            ub = ar.bf16(T)
            aas = [ar.f32(T) for _ in range(2)]
            bbs = [ar.f32(T) for _ in range(2)]
            hf_ = ar.f32(T)
            hb_ = ar.f32(T)
            hh = [hf_, hb_]
            fin = ar.f32(2, 4, 8)
            DMA("pool", lw.rearrange("p a b -> p (a b)"), lruw_d[l], "d_c5", writes=["lw"])
            flag = vecs[:, V_FLAG:V_FLAG + 1]

            def seg(ap):
                return ap.rearrange("p (s t) -> p s t", t=256)

            for n in range(4):
                lxn = lx[:, n, :]
                w = lambda j: vcol(V_CONVW + j * 4 + n)
                we = lambda j: cwe[:, j * 4 + n:j * 4 + n + 1]
                TS("dve", u, lxn, w(2), vcol(V_CONVB + n), ALU.mult, ALU.add, reads=["vecs", "hsum"], writes=["u0"])
                us, ls = seg(u), seg(lxn)
                STT(us[:, :, 2:256], ls[:, :, 0:254], w(0), us[:, :, 2:256], ALU.mult, ALU.add, reads=["u0"], writes=["u1"])
                STT(us[:, :, 1:256], ls[:, :, 0:255], w(1), us[:, :, 1:256], ALU.mult, ALU.add, reads=["u1"], writes=["u2"])
                STT(us[:, :, 0:255], ls[:, :, 1:256], w(3), us[:, :, 0:255], ALU.mult, ALU.add, reads=["u2"], writes=["u3"])
                STT(us[:, 1:8, 0:2], ls[:, 0:7, 254:256], we(0), us[:, 1:8, 0:2], ALU.mult, ALU.add,
                    reads=["u3", "cwe"], writes=["u4"])
                STT(us[:, 1:8, 0:1], ls[:, 0:7, 255:256], we(1), us[:, 1:8, 0:1], ALU.mult, ALU.add, reads=["u4"], writes=["u5"])
                STT(us[:, 0:7, 255:256], ls[:, 1:8, 0:1], we(3), us[:, 0:7, 255:256], ALU.mult, ALU.add,
                    reads=["u5"], writes=["u"])
                CP("act", ub, u, reads=["u"], writes=["ub"])
                for d in range(2):
                    for tt in range(NTT):
                        tsl = slice(tt * TT, (tt + 1) * TT)
                        b1_ = next_bank()
                        MM(bank(b1_), lw[:, d * 4 + n, :], ub[:, tsl], True, True, reads=["lw", "ub"], writes=[("ps", b1_)])
                        b2_ = next_bank()
                        MM(bank(b2_), lw[:, 8 + d * 4 + n, :], ub[:, tsl], True, True, reads=["lw", "ub"], writes=[("ps", b2_)])
                        ACT(aas[d][:, tsl], bank(b1_), AF.Sigmoid, reads=[("ps", b1_), "vecs", ("scan", d)],
                            writes=[("aa", d, tt)], bias=vcol(V_BA + d * 4 + n))
                        ACT(bbs[d][:, tsl], bank(b2_), AF.Sigmoid, reads=[("ps", b2_), "vecs", ("scan", d)],
                            writes=[("bb", d, tt)], bias=vcol(V_BX + d * 4 + n))
                for d in range(2):
                    ACT(aas[d], aas[d], AF.Exp, reads=[("aa", d, t_) for t_ in range(NTT)] + ["lam8"], writes=[("aE", d)],
                        scale=lam8[:, d * 4 + n:d * 4 + n + 1])
                    TT_("dve", hh[d], aas[d], aas[d], ALU.mult, reads=[("aE", d), "hsum"], writes=[("a2", d)])
                for d in range(2):
                    ACT(hh[d], hh[d], AF.Sqrt, reads=[("a2", d)], writes=[("sq", d)], bias=1.0, scale=-1.0)
                for d in range(2):
                    TT_("dve", bbs[d], bbs[d], hh[d], ALU.mult, reads=[("bb", d, t_) for t_ in range(NTT)] + [("sq", d)],
                        writes=[("b1", d)])
                    TT_("dve", bbs[d], bbs[d], u, ALU.mult, reads=[("b1", d), "u"], writes=[("b2", d)])
                    asg = seg(aas[d])
                    h0 = vcol(V_H0 + d * 4 + n)
                    if d == 0:
                        TS("dve", asg[:, 1:8, 0:1], asg[:, 1:8, 0:1], flag, None, ALU.mult, reads=[("aE", 0), "vecs"], writes=[("aam", 0)])
                        S.op("dve", lambda e, h0=h0: e.tensor_tensor_scan(hf_, aas[0], bbs[0], h0, ALU.mult, ALU.add),
                             [("aam", 0), "vecs", ("b2", 0), ("sq", 0)], [("scan", 0), "hf"])
                        CP("act", fin[:, 0, n, :], seg(hf_)[:, :, 255], reads=["hf"], writes=[("fin", 0, n)])
                    else:
                        TS("dve", asg[:, 0:7, 255:256], asg[:, 0:7, 255:256], flag, None, ALU.mult,
                           reads=[("aE", 1), "vecs"], writes=[("aam", 1)])
                        S.op("dve", lambda e, h0=h0: e.tensor_tensor_scan(hb_[:, ::-1], aas[1][:, ::-1], bbs[1][:, ::-1], h0,
                                                                          ALU.mult, ALU.add),
                             [("aam", 1), "vecs", ("b2", 1), ("sq", 1)], [("scan", 1), "hb"])
                        CP("act", fin[:, 1, n, :], seg(hb_)[:, :, 0], reads=["hb"], writes=[("fin", 1, n)])
                TT_("dve", hf_, hf_, hb_, ALU.add, reads=["hf", "hb", ("fin", 0, n), ("fin", 1, n)], writes=["hsum0"])
                TT_("dve", MX[:, 8 + n, :], hf_, MX[:, 8 + n, :], ALU.mult, reads=["hsum0"], writes=["hsum", ("mxl", n)])
            for d in range(2):
                DMA("sp", st_o[l, d].rearrange("n p s -> p n s"), fin[:, d, :, :], "d_so",
                    reads=[("fin", d, n) for n in range(4)])

            chk('B2', l)
            S.barrier()
            ar.reset(prefix_off)
            rstd2 = ar.f32(T)
            xin_ = [ar.f32(TT) for _ in range(4)]
            xo = [ar.f32(TT) for _ in range(4)]
            sq2 = [ar.bf16(TT) for _ in range(2)]
            for hf in range(2):
                ssb = [6, 7]
                steps = [(cb, t2) for cb in range(NCH) for t2 in range(2)]

                def cload(si):
                    cb, t2 = steps[si]
                    tt = hf * 2 + t2
                    k = si % 4
                    DMA("sp", xin_[k], x_in[cb * P:(cb + 1) * P, tt * TT:(tt + 1) * TT], "d_ld%d" % k,
                        reads=[("x", l, cb, tt)], writes=[("xin", k)])

                for si in range(3):
                    cload(si)
                slot = rkey = None
                pend_c = [None]
                for si, (cb, t2) in enumerate(steps):
                    if si + 3 < len(steps):
                        cload(si + 3)
                    if t2 == 0 and cb % 2 == 0:
                        slot, rkey = ring_next(l * NS_LAYER + NS_MOD + NS_IN + cb // 2)
                    tt = hf * 2 + t2
                    tsl = slice(tt * TT, (tt + 1) * TT)
                    b_ = next_bank(0, 6)
                    for kc in range(NCH):
                        MM(bank(b_), slot[:, (cb % 2) * 16 + kc, :], MX[:, kc, tsl], kc == 0, kc == NCH - 1,
                           reads=[rkey], writes=[("ps", b_)])
                    k = si % 4
                    STT(xo[k], bank(b_), mod(2)[:, cb:cb + 1], xin_[k], ALU.mult, ALU.add,
                        reads=[("ps", b_), ("xin", k), ("mods", l)], writes=[("xo", k)])
                    DMA("act", xsB[cb * P:(cb + 1) * P, tsl], xo[k], "d_st%d" % k, reads=[("xo", k)], writes=[("xB", cb, tt)])
                    ACT(sq2[si % 2], xo[k], AF.Square, reads=[("xo", k)], writes=[("sq2", si % 2)])
                    if pend_c[0] is not None:
                        pend_c[0]()
                    pend_c[0] = (lambda si=si, t2=t2, cb=cb: MM(bank(ssb[t2]), ones_b, sq2[si % 2], cb == 0, cb == NCH - 1,
                                                               reads=[("sq2", si % 2), "cm"], writes=[("ps", ssb[t2])]))
                pend_c[0]()
                pend_c[0] = None
                for t2 in range(2):
                    tt = hf * 2 + t2
                    tsl = slice(tt * TT, (tt + 1) * TT)
                    ACT(rstd2[:, tsl], bank(ssb[t2]), AF.Sqrt, reads=[("ps", ssb[t2])], writes=[("r2a", tt)],
                        bias=EPS, scale=1.0 / D)
                    S.op("dve", lambda e, tsl=tsl: e.reciprocal(rstd2[:, tsl], rstd2[:, tsl]), [("r2a", tt)], [("r2", tt)])

            chk('C', l)
            S.barrier()
            ar.reset(prefix_off + T)
            HT = 2 * TT
            stg2 = [ar.f32(TT) for _ in range(4)]
            tmp2 = [ar.f32(TT) for _ in range(2)]
            sg = [ar.f32(TT) for _ in range(2)]
            xo2 = [ar.f32(TT) for _ in range(4)]
            sq3 = [ar.bf16(TT) for _ in range(2)]
            stg3 = [ar.f32(TT) for _ in range(4)]
            end_small = ar.off
            ar.reset(0)
            h2 = ar.bf16(NCH, HT)
            actb_a = None
            if ar.off + (NFF * HT) // 2 <= prefix_off:
                actb = ar.bf16(NFF, HT)
            else:
                n_pre = (prefix_off - ar.off) * 2 // HT
                act_pre = ar.bf16(n_pre, HT)
                ar.reset(end_small)
                act_post = ar.bf16(NFF - n_pre, HT)
                actb = None
            if actb is None:
                def act_at(j):
                    return act_pre[:, j, :] if j < n_pre else act_post[:, j - n_pre, :]
            else:
                def act_at(j):
                    return actb[:, j, :]
            last = (l == nl - 1)
            x_out = xsA
            modgen = mod_steps(l + 1) if not last else None

            def mod_step():
                nonlocal modgen
                if modgen is not None:
                    try:
                        next(modgen)
                    except StopIteration:
                        modgen = None
            def h2_steps(hf):
                steps = [(c, t2) for t2 in range(2) for c in range(NCH)]

                def dload(si):
                    c, t2 = steps[si]
                    tt = hf * 2 + t2
                    k = si % 4
                    DMA("sp", stg3[k], xsB[c * P:(c + 1) * P, tt * TT:(tt + 1) * TT], "d_h%d" % k,
                        reads=[("xB", c, tt)], writes=[("stg3", k)])

                for si in range(3):
                    dload(si)
                for si, (c, t2) in enumerate(steps):
                    if si + 3 < len(steps):
                        dload(si + 3)
                    tt = hf * 2 + t2
                    k = si % 4
                    STT(tmp2[si % 2], stg3[k], gs2[:, c:c + 1], rstd2[:, tt * TT:(tt + 1) * TT], ALU.mult, ALU.mult,
                        reads=[("stg3", k), "gs2", ("r2", tt)], writes=[("tmp2", si % 2)])
                    ACT(h2[:, c, t2 * TT:(t2 + 1) * TT], tmp2[si % 2], AF.Identity,
                        reads=[("tmp2", si % 2), ("mods", l)], writes=[("h2", c, t2)], bias=mod(3)[:, c:c + 1])
                    yield si

            h2gen1 = h2_steps(1)
            for hf in range(2):
                if hf == 0:
                    for _ in h2_steps(0):
                        pass
                else:
                    for _ in h2gen1:
                        pass
                for j in range(NFF):
                    slot, rkey = ring_next(l * NS_LAYER + NS_MOD + NS_IN + NS_OUT + j)
                    for t2 in range(2):
                        bg = next_bank(0, 6)
                        bu = next_bank(0, 6)
                        hr = [("h2", c, t2) for c in range(NCH)]
                        for kc in range(NCH):
                            MM(bank(bg), slot[:, kc, :], h2[:, kc, t2 * TT:(t2 + 1) * TT], kc == 0, kc == NCH - 1,
                               reads=[rkey] + (hr if kc == 0 else []), writes=[("ps", bg)])
                        for kc in range(NCH):
                            MM(bank(bu), slot[:, 16 + kc, :], h2[:, kc, t2 * TT:(t2 + 1) * TT], kc == 0, kc == NCH - 1,
                               reads=[rkey], writes=[("ps", bu)])
                        k2 = (j * 2 + t2) % 2
                        ACT(sg[k2], bank(bg), AF.Silu, reads=[("ps", bg)], writes=[("sg", k2)])
                        TT_("dve", act_at(j)[:, t2 * TT:(t2 + 1) * TT], bank(bu), sg[k2], ALU.mult,
                            reads=[("ps", bu), ("sg", k2)], writes=[("act", j, t2)])
                    if j % 2 == 1 or j in (0, 10, 20):
                        mod_step()
                steps = [(m, t2) for m in range(NCH) for t2 in range(2)]

                def eload(si):
                    m, t2 = steps[si]
                    tt = hf * 2 + t2
                    k = si % 4
                    DMA("sp", stg2[k], xsB[m * P:(m + 1) * P, tt * TT:(tt + 1) * TT], "d_ld%d" % k,
                        reads=[("xB", m, tt)], writes=[("stg2", k)])

                for si in range(3):
                    eload(si)
                ssb = [6, 7]
                cur_s = {"idx": -1, "slot": None, "rkey": None}
                pend_d = []
                for m in range(NCH + 1):
                    if m == NCH:
                        for f_ in pend_d:
                            f_()
                        pend_d = []
                        break
                    bks = [next_bank(0, 6), next_bank(0, 6)]
                    for kc in range(NFF):
                        if kc == 8 and pend_d:
                            for f_ in pend_d:
                                f_()
                            pend_d = []
                        bidx = m * NFF + kc
                        sidx = bidx // 32
                        if sidx != cur_s["idx"]:
                            cur_s["slot"], cur_s["rkey"] = ring_next(l * NS_LAYER + NS_MOD + NS_IN + NS_OUT + NS_W1 + sidx)
                            cur_s["idx"] = sidx
                        slot, rkey = cur_s["slot"], cur_s["rkey"]
                        for t2 in range(2):
                            MM(bank(bks[t2]), slot[:, bidx % 32, :], act_at(kc)[:, t2 * TT:(t2 + 1) * TT],
                               kc == 0, kc == NFF - 1, reads=[rkey, ("act", kc, t2)], writes=[("ps", bks[t2])])
                    for t2 in range(2):
                        si = m * 2 + t2
                        if si + 3 < len(steps):
                            eload(si + 3)
                        tt = hf * 2 + t2
                        tsl = slice(tt * TT, (tt + 1) * TT)
                        b_ = bks[t2]
                        k = si % 4
                        STT(xo2[k], bank(b_), mod(5)[:, m:m + 1], stg2[k], ALU.mult, ALU.add,
                            reads=[("ps", b_), ("stg2", k), ("mods", l)], writes=[("xo2", k)])
                        DMA("act", x_out[m * P:(m + 1) * P, tsl], xo2[k], "d_st%d" % k, reads=[("xo2", k)],
                            writes=[("x", l + 1, m, tt)])
                        if last:
                            ACT(sq3[si % 2], xo2[k], AF.Square, reads=[("xo2", k)], writes=[("sq3", si % 2)])
                            pend_d.append(lambda si=si, t2=t2, m=m: MM(bank(ssb[t2]), ones_b, sq3[si % 2], m == 0, m == NCH - 1,
                                                                       reads=[("sq3", si % 2), "cm"], writes=[("ps", ssb[t2])]))
                    if hf == 0:
                        for _ in range(2):
                            next(h2gen1, None)
                if last:
                    for t2 in range(2):
                        tt = hf * 2 + t2
                        tsl = slice(tt * TT, (tt + 1) * TT)
                        ACT(rstd2[:, tsl], bank(ssb[t2]), AF.Sqrt, reads=[("ps", ssb[t2])], writes=[("r3a", tt)],
                            bias=EPS, scale=1.0 / D)
                        S.op("dve", lambda e, tsl=tsl: e.reciprocal(rstd2[:, tsl], rstd2[:, tsl]), [("r3a", tt)], [("r3", tt)])
                    steps = [(c, t2) for t2 in range(2) for c in range(NCH)]

                    def fload(si):
                        c, t2 = steps[si]
                        tt = hf * 2 + t2
                        k = si % 4
                        DMA("sp", stg2[k], xsA[c * P:(c + 1) * P, tt * TT:(tt + 1) * TT], "d_ld%d" % k,
                            reads=[("x", l + 1, c, tt)], writes=[("stg2", k)])

                    for si in range(3):
                        fload(si)
                    for si, (c, t2) in enumerate(steps):
                        if si + 3 < len(steps):
                            fload(si + 3)
                        tt = hf * 2 + t2
                        tsl = slice(tt * TT, (tt + 1) * TT)
                        k = si % 4
                        STT(xo2[k], stg2[k], vecs[:, V_NFIN + c:V_NFIN + c + 1], rstd2[:, tsl], ALU.mult, ALU.mult,
                            reads=[("stg2", k), "vecs", ("r3", tt)], writes=[("xo2", k)])
                        DMA("sp", yT[c * P:(c + 1) * P, tsl], xo2[k], "d_st%d" % k, reads=[("xo2", k)])
            while modgen is not None:
                mod_step()

        def chk(tag, l=0):
            if STOP_AT == tag or STOP_AT == "%s%d" % (tag, l):
                raise _Stop()

        try:
            chk("pro")
            for l in range(nl):
                layer(l)
        except _Stop:
            pass
        S.barrier()
        final_waits = [(k, v) for k, v in S.dval.items()]
        S.stream["sp"].append((final_waits, None, None))

        def replay(eng_name):
            def run(e):
                for waits, fn, dsem in S.stream[eng_name]:
                    for key, val in waits:
                        e.wait_ge(semh[key], val)
                    if fn is None:
                        continue
                    ins = fn(e)
                    if dsem is not None:
                        ins.then_inc(semh[dsem], 16)
                    else:
                        ins.then_inc(semh[eng_name], 1)
            return run

        block.tensor(replay("pe"))
        block.vector(replay("dve"))
        block.scalar(replay("act"))
        block.gpsimd(replay("pool"))
        block.sync(replay("sp"))
    return nc, ring_state["rec"]


def _fm(v):
    v = np.asarray(v, np.float32).reshape(-1, P)
    return np.ascontiguousarray(v.T)


def _tile_w(w):
    K, N = w.shape
    kc, cb = K // P, N // P
    b = w.reshape(kc, P, cb, P).transpose(2, 0, 1, 3)
    b = b.reshape(cb * kc, P, P)
    nb = b.shape[0]
    assert nb % 32 == 0
    return np.ascontiguousarray(b.reshape(nb // 32, 32, P, P).transpose(0, 2, 1, 3).reshape(nb // 32, P, 32 * P))


def _tile_w1(w1):
    K, N = w1.shape
    ga = w1[:, :N // 2].reshape(NCH, P, NFF, P)
    up = w1[:, N // 2:].reshape(NCH, P, NFF, P)
    s = np.stack([ga, up], axis=0)
    s = s.transpose(3, 2, 0, 1, 4)
    return np.ascontiguousarray(s.reshape(NFF, P, 32 * P))


def _rope_tables(seg_len, rope):
    if not rope:
        return np.ones((P, T), np.float32), np.zeros((P, T), np.float32)
    t = np.arange(T)
    row = (t // 64).astype(np.float32)
    col = (t % 64).astype(np.float32)
    rd = HD // 4
    inv = (np.float32(10000.0) ** (-np.arange(rd, dtype=np.float32) / np.float32(rd))).astype(np.float32)
    ang = np.zeros((T, 2, 2, rd), np.float32)
    ang[:, 0, :, :] = (row[:, None] * inv[None, :])[:, None, :]
    ang[:, 1, :, :] = (col[:, None] * inv[None, :])[:, None, :]
    ang = ang.reshape(T, HD)
    return np.ascontiguousarray(np.cos(ang).T.astype(np.float32)), np.ascontiguousarray(np.sin(ang).T.astype(np.float32))


def _band_bias(sample):
    NEG = np.float32(-1e30)
    b = np.zeros((P, NBK, 384), np.float32)
    i = np.arange(P)[:, None]
    j = np.arange(P)[None, :]
    for qb in range(NBK):
        if sample:
            prev = np.where(j >= i, 0.0, NEG) if qb >= 1 else np.full((P, P), NEG)
            nxt = np.where(j <= i, 0.0, NEG) if qb <= NBK - 2 else np.full((P, P), NEG)
        else:
            prev = np.zeros((P, P)) if qb % 2 == 1 else np.full((P, P), NEG)
            nxt = np.zeros((P, P)) if qb % 2 == 0 else np.full((P, P), NEG)
        b[:, qb, 0:128] = prev
        b[:, qb, 256:384] = nxt
    return b.reshape(P, NBK * 384)


def _pool_tables(seg_len):
    wins = (2, 4, 8, 16)
    pint = np.zeros((P, 4, 112), np.float32)
    pedge = np.zeros((P, NBK, 4, 4, 8), np.float32)
    for g, w in enumerate(wins):
        M = np.zeros((T, T), np.float32)
        for s0 in range(0, T, seg_len):
            for tl in range(seg_len):
                lo = min(max(tl - w // 2, 0), seg_len)
                hi = min(max(tl + w // 2, 0), seg_len)
                to = s0 + tl
                M[s0 + lo:s0 + hi, to] = np.float32(1.0) / np.float32(hi - lo)
                M[to, to] -= 1.0
        blk = M[128:256, 128:256] if seg_len >= 384 else None
        for bi in range(NBK):
            d = M[bi * P:(bi + 1) * P, bi * P:(bi + 1) * P]
            if bi == 0:
                pint[:, g, :] = d[:, 8:120]
            else:
                assert np.array_equal(pint[:, g, :], d[:, 8:120])
            pedge[:, bi, g, 1, :] = d[:, 0:8]
            pedge[:, bi, g, 2, :] = d[:, 120:128]
            if bi > 0:
                pedge[:, bi, g, 0, :] = M[(bi - 1) * P:bi * P, bi * P:bi * P + 8]
            if bi < NBK - 1:
                pedge[:, bi, g, 3, :] = M[(bi + 1) * P:(bi + 2) * P, bi * P + 120:(bi + 1) * P]
    return pint.reshape(P, 4 * 112), pedge.reshape(P, NBK * 4 * 4 * 8)


def _cmat():
    ones = np.ones((P, P), np.float32)
    R = np.zeros((P, P), np.float32)
    for m in range(P):
        if (m // 32) % 2 == 0:
            R[m + 32, m] = -1.0
        else:
            R[m - 32, m] = 1.0
    I = np.eye(P, dtype=np.float32)
    return np.ascontiguousarray(np.concatenate([ones, R, I], axis=1))


def prepare_inputs(inp, nl=DEPTH):
    f = lambda k: np.asarray(inp[k], np.float32)
    x_prompt, x_sample = f("x_prompt"), f("x_sample")
    cache_k, cache_v, state_lru = f("cache_k"), f("cache_v"), f("state_lru")
    c, c_ctx = f("c"), f("c_ctx")
    slots = []
    for l in range(nl):
        slots.append(_tile_w(f("mod_w")[l]))
        slots.append(_tile_w(f("w_in")[l]))
        slots.append(_tile_w(f("w_out")[l]))
        slots.append(_tile_w1(f("ffn_w1")[l]))
        slots.append(_tile_w(f("ffn_w2")[l]))
    wall = np.concatenate(slots, axis=0)
    assert wall.shape[0] == nl * NS_LAYER, wall.shape
    lruw = np.zeros((DEPTH, P, 16, P), np.float32)
    wa, wx = f("lru_wa"), f("lru_wx")
    for l in range(DEPTH):
        for d in range(2):
            for n in range(4):
                lruw[l, :, d * 4 + n, :] = wa[l, d, n]
                lruw[l, :, 8 + d * 4 + n, :] = wx[l, d, n]
    lruw = lruw.reshape(DEPTH, P, 16 * P)
    poolw = np.ascontiguousarray(f("pool_w").transpose(0, 2, 1, 3)).reshape(DEPTH, P, 4 * P)
    cmat = _cmat()
    tabs = {}
    for kind, seg_len in (("p", 256), ("s", 2048)):
        cosT, sinT = _rope_tables(seg_len, kind == "s")
        pint, pedge = _pool_tables(seg_len)
        tabs[kind] = dict(cosT=cosT, sinT=sinT, biasb=_band_bias(kind == "s"), pint=pint, pedge=pedge)

    def vec_common():
        v = np.zeros((P, NV), np.float32)
        for l in range(DEPTH):
            b = l * V_LAYER
            v[:, b + V_MODB:b + V_MODB + 96] = _fm(f("mod_b")[l])
            v[:, b + V_NMIX:b + V_NMIX + 16] = _fm(f("norm_mix")[l])
            v[:, b + V_NFFN:b + V_NFFN + 16] = _fm(f("norm_ffn")[l])
            v[:, b + V_CONVW:b + V_CONVW + 16] = _fm(f("conv_w")[l])
            v[:, b + V_CONVB:b + V_CONVB + 4] = _fm(f("conv_b")[l])
            v[:, b + V_BA:b + V_BA + 8] = _fm(f("lru_ba")[l])
            v[:, b + V_BX:b + V_BX + 8] = _fm(f("lru_bx")[l])
            v[:, b + V_LAM:b + V_LAM + 8] = _fm(f("lru_lambda")[l])
            v[:, b + V_PSC:b + V_PSC + 4] = _fm(f("pool_scale")[l])
            v[:, b + V_SINK:b + V_SINK + 8] = f("attn_sink")[l][None, :]
        v[:, V_NFIN:V_NFIN + 16] = _fm(f("norm_final"))
        return v

    vbase = vec_common()
    zeros_ck = np.zeros((DEPTH, P, NKV * 512), np.float32)
    zeros_cv = np.zeros((DEPTH, 512, NKV * HD), np.float32)
    in_maps = []
    for ci in range(8):
        if ci in (4, 5):
            b = ci - 4
            kind = "s"
            x = x_sample[b]
            cond = c[b]
            v = vbase.copy()
            for l in range(DEPTH):
                v[:, l * V_LAYER + V_H0:l * V_LAYER + V_H0 + 8] = _fm(state_lru[b, l])
            v[:, V_FLAG] = 1.0
            v[:, V_CTXB] = 0.0
            ckT = np.ascontiguousarray(cache_k[b].transpose(0, 3, 2, 1)).reshape(DEPTH, P, NKV * 512)
            cv = np.ascontiguousarray(cache_v[b]).reshape(DEPTH, 512, NKV * HD)
        else:
            pc = ci if ci < 4 else ci - 6
            kind = "p"
            x = x_prompt[pc * 8:(pc + 1) * 8].reshape(T, D)
            cond = c_ctx
            v = vbase.copy()
            v[:, V_FLAG] = 0.0
            v[:, V_CTXB] = -1e30
            ckT, cv = zeros_ck, zeros_cv
        m = dict(xT=np.ascontiguousarray(x.T), cond=_fm(cond), vecs=v, wall=wall, lruw=lruw, poolw=poolw,
                 ckT=ckT, cv=cv, cmat=cmat)
        m.update(tabs[kind])
        in_maps.append(m)
    return in_maps


_PROG = {}


def kernel(**inputs):
    return run_step(inputs, DEPTH)


def run_step(inputs, nl, trace=False):
    in_maps = prepare_inputs(inputs, nl)
    if nl not in _PROG:
        _PROG[nl] = build_program(nl)
    nc = _PROG[nl]
    res = run_bass_kernel_spmd(nc, in_maps, core_ids=list(range(8)), **({'trace': True} if trace else {}))
    _PROG['last'] = res
    r = res.results
    B, SEQ = 32, 256
    y_prompt = np.zeros((B, SEQ, D), np.float32)
    new_k = np.zeros((B, DEPTH, SEQ, NKV, HD), np.float32)
    new_v = np.zeros((B, DEPTH, SEQ, NKV, HD), np.float32)
    new_s = np.zeros((B, DEPTH, 2, 512), np.float32)
    y_sample = np.zeros((2, T, D), np.float32)
    for ci in range(4):
        o = r[ci]
        sl = slice(ci * 8, (ci + 1) * 8)
        y_prompt[sl] = o["yT"].T.reshape(8, SEQ, D)
        new_k[sl] = o["kT_o"].reshape(DEPTH, NKV, HD, 8, SEQ).transpose(3, 0, 4, 1, 2)
        new_v[sl] = o["v_o"].reshape(DEPTH, 8, SEQ, NKV, HD).transpose(1, 0, 2, 3, 4)
        new_s[sl] = o["st_o"].transpose(4, 0, 1, 2, 3).reshape(8, DEPTH, 2, 512)
    for b in range(2):
        y_sample[b] = r[4 + b]["yT"].T
    return (y_prompt, y_sample, new_k, new_v, new_s)
```

```python
import numpy as np
import concourse.bass as bass
import concourse.mybir as mybir
from concourse.bass_utils import run_bass_kernel_spmd

F32 = mybir.dt.float32
BF16 = mybir.dt.bfloat16
AF = mybir.ActivationFunctionType
ALU = mybir.AluOpType
AX = mybir.AxisListType

P = 128
D = 2048
NCH = 16
T = 2048
TT = 512
NTT = 4
NBK = 16
DEPTH = 4
HD = 128
NH = 8
NKV = 2
NFF = 44
SCALE = float(HD) ** -0.5
EPS = 1e-6
NSLOT = 4
SLOT_COLS = 4096
NS_MOD, NS_IN, NS_OUT, NS_W1, NS_W2 = 48, 12, 8, 44, 22
NS_LAYER = NS_MOD + NS_IN + NS_OUT + NS_W1 + NS_W2
V_MODB, V_NMIX, V_NFFN, V_CONVW, V_CONVB, V_BA, V_BX, V_LAM, V_PSC, V_H0, V_SINK = (
    0, 96, 112, 128, 144, 148, 156, 164, 172, 176, 184)
V_LAYER = 192
V_NFIN = DEPTH * V_LAYER
V_FLAG = V_NFIN + 16
V_CTXB = V_FLAG + 1
NV = V_CTXB + 1

ENGS = ("pe", "dve", "act", "pool", "sp")


class Sched:
    def __init__(self):
        self.stream = {e: [] for e in ENGS}
        self.cnt = {e: 0 for e in ENGS}
        self.waited = {e: {} for e in ENGS}
        self.res = {}
        self.dval = {}

    def _need(self, eng, deps):
        wd = self.waited[eng]
        out = []
        for key, val in deps:
            if eng == "pe" and key == "pe":
                continue
            if wd.get(key, 0) >= val:
                continue
            wd[key] = val
            out.append((key, val))
        return out

    def op(self, eng, fn, reads=(), writes=(), dsem=None):
        deps = []
        for r in reads:
            st = self.res.get(r)
            if st is not None:
                if st[0] is not None:
                    deps.append(st[0])
                if isinstance(r, tuple) and r[0] == "ps":
                    deps.extend((k, v) for k, v in st[1].items() if k != eng)
        for w in writes:
            st = self.res.get(w)
            if st is not None:
                if st[0] is not None:
                    deps.append(st[0])
                deps.extend(st[1].items())
        if dsem is not None:
            prev = self.dval.get(dsem, 0)
            if prev:
                deps.append((dsem, prev))
            val = prev + 16
            self.dval[dsem] = val
            ev = (dsem, val)
        else:
            self.cnt[eng] += 1
            ev = (eng, self.cnt[eng])
        waits = self._need(eng, deps)
        self.stream[eng].append((waits, fn, dsem))
        for r in reads:
            st = self.res.get(r)
            if st is None:
                self.res[r] = [None, {ev[0]: ev[1]}]
            else:
                if st[1].get(ev[0], 0) < ev[1]:
                    st[1][ev[0]] = ev[1]
        for w in writes:
            self.res[w] = [ev, {}]
        return ev

    def barrier(self):
        deps = [(e, self.cnt[e]) for e in ("pe", "dve", "act", "pool") if self.cnt[e]]
        deps += [(k, v) for k, v in self.dval.items() if not k.startswith("d_ring")]
        for e in ENGS:
            waits = self._need(e, deps)
            if waits:
                self.stream[e].append((waits, None, None))
        self.res = {k: v for k, v in self.res.items() if isinstance(k, tuple) and k[0] == "ring"}


class Arena:
    def __init__(self, ap, ncols):
        self.ap = ap
        self.ncols = ncols
        self.off = 0

    def reset(self, off=0):
        self.off = off

    def _take(self, n4):
        assert self.off + n4 <= self.ncols, ("arena overflow", self.off, n4, self.ncols)
        v = self.ap[:, self.off:self.off + n4]
        self.off += n4
        return v

    @staticmethod
    def _shape(v, shape):
        if len(shape) == 1:
            return v
        if len(shape) == 2:
            return v.rearrange("p (a b) -> p a b", a=shape[0])
        if len(shape) == 3:
            return v.rearrange("p (a b c) -> p a b c", a=shape[0], b=shape[1])
        raise ValueError(shape)

    def f32(self, *shape):
        n = int(np.prod(shape))
        return self._shape(self._take(n), shape)

    def bf16(self, *shape):
        n = int(np.prod(shape))
        assert n % 2 == 0
        return self._shape(self._take(n // 2).bitcast(BF16), shape)


class _Stop(Exception):
    pass


STOP_AT = None


def build_program(nl=DEPTH, debug=False):
    _, plan = _build(nl, None)
    nc, plan2 = _build(nl, plan)
    assert plan2 == plan
    return nc


def _build(nl, plan_in):
    nc = bass.Bass("TRN2", target_bir_lowering=False)
    dt = nc.dram_tensor
    xT = dt("xT", [D, T], F32, kind="ExternalInput").ap()
    cond_d = dt("cond", [P, NCH], F32, kind="ExternalInput").ap()
    vecs_d = dt("vecs", [P, NV], F32, kind="ExternalInput").ap()
    wall = dt("wall", [nl * NS_LAYER, P, SLOT_COLS], F32, kind="ExternalInput").ap()
    lruw_d = dt("lruw", [DEPTH, P, 16 * 128], F32, kind="ExternalInput").ap()
    poolw_d = dt("poolw", [DEPTH, P, 4 * 128], F32, kind="ExternalInput").ap()
    ckT_d = dt("ckT", [DEPTH, P, NKV * 512], F32, kind="ExternalInput").ap()
    cv_d = dt("cv", [DEPTH, 512, NKV * HD], F32, kind="ExternalInput").ap()
    cos_d = dt("cosT", [P, T], F32, kind="ExternalInput").ap()
    sin_d = dt("sinT", [P, T], F32, kind="ExternalInput").ap()
    bias_d = dt("biasb", [P, NBK * 384], F32, kind="ExternalInput").ap()
    pint_d = dt("pint", [P, 4 * 112], F32, kind="ExternalInput").ap()
    pedge_d = dt("pedge", [P, NBK * 4 * 4 * 8], F32, kind="ExternalInput").ap()
    cmat_d = dt("cmat", [P, 3 * 128], F32, kind="ExternalInput").ap()

    yT = dt("yT", [D, T], F32, kind="ExternalOutput").ap()
    kT_o = dt("kT_o", [DEPTH, NKV, P, T], F32, kind="ExternalOutput").ap()
    v_o = dt("v_o", [DEPTH, T, NKV * HD], F32, kind="ExternalOutput").ap()
    st_o = dt("st_o", [DEPTH, 2, 4, P, 8], F32, kind="ExternalOutput").ap()
    xsA = dt("xsA", [D, T], F32, kind="Internal").ap()
    xsB = dt("xsB", [D, T], F32, kind="Internal").ap()

    S = Sched()
    ARENA_COLS = 42752
    PERS_COLS = 1664
    RING_COLS = NSLOT * SLOT_COLS // 2

    sem_names = ["pe", "dve", "act", "pool"] + ["d_ring%d" % i for i in range(NSLOT)]
    dyn_sems = ["d_ld%d" % i for i in range(6)] + ["d_st%d" % i for i in range(4)] + [
        "d_c%d" % i for i in range(12)] + ["d_h%d" % i for i in range(4)] + ["d_vo0", "d_vo1", "d_ko0", "d_ko1", "d_so"]
    sem_names += dyn_sems

    import contextlib
    with contextlib.ExitStack() as es:
        arena_t = es.enter_context(nc.sbuf_tensor("arena", [P, ARENA_COLS], F32))
        pers_t = es.enter_context(nc.sbuf_tensor("pers", [P, PERS_COLS], F32))
        ring_t = es.enter_context(nc.sbuf_tensor("ring", [P, NSLOT * SLOT_COLS], BF16))
        ps_t = es.enter_context(nc.psum_tensor("ps", [P, 8 * 512], F32))
        semh = {n: es.enter_context(nc.semaphore(n)) for n in sem_names}
        block = es.enter_context(nc.Block())

        ar = Arena(arena_t[:, :], ARENA_COLS)
        pr = Arena(pers_t[:, :], PERS_COLS)
        ring = [ring_t[:, i * SLOT_COLS:(i + 1) * SLOT_COLS] for i in range(NSLOT)]

        def bank(b):
            return ps_t[:, b * 512:(b + 1) * 512]

        def bank_bf(b):
            return ps_t[:, b * 512:(b + 1) * 512].bitcast(BF16)

        vecs = pr.f32(NV)
        cond = pr.f32(NCH)
        scb = pr.bf16(NCH)
        mods = pr.f32(DEPTH, 96)
        cm = pr.bf16(3, 128)
        ones_b, rot_b, ident_b = cm[:, 0, :], cm[:, 1, :], cm[:, 2, :]
        gs1 = pr.f32(NCH)
        gs2 = pr.f32(NCH)
        lam8 = pr.f32(8)
        cwe = pr.f32(16)
        cwq = pr.bf16(16)
        sm = pr.f32(64)
        rstdF = None

        def DMA(q, out, in_, dsem, reads=(), writes=()):
            return S.op(q, lambda e: e.dma_start(out=out, in_=in_), reads, writes, dsem=dsem)

        def MM(out, lhsT, rhs, start, stop, reads=(), writes=()):
            for r_ in reads:
                if isinstance(r_, tuple) and r_[0] == "ring":
                    assert r_[1] == (ring_state["next"] - 1) % NSLOT, "stale ring slot"
            return S.op("pe", lambda e: e.matmul(out, lhsT, rhs, start=start, stop=stop), reads, writes)

        def TR(out, in_, reads=(), writes=()):
            return S.op("pe", lambda e: e.transpose(out, in_, ident_b), reads, writes)

        def ACT(out, in_, func, reads=(), writes=(), bias=None, scale=None):
            kw = {}
            if bias is not None:
                kw["bias"] = bias
            if scale is not None:
                kw["scale"] = scale
            return S.op("act", lambda e: e.activation(out, in_, func, **kw), reads, writes)

        def TT_(eng, out, in0, in1, op, reads=(), writes=()):
            return S.op(eng, lambda e: e.tensor_tensor(out, in0, in1, op), reads, writes)

        def TS(eng, out, in0, s1, s2, op0, op1=None, reads=(), writes=()):
            if op1 is None:
                return S.op(eng, lambda e: e.tensor_scalar(out, in0, s1, None, op0), reads, writes)
            return S.op(eng, lambda e: e.tensor_scalar(out, in0, s1, s2, op0, op1), reads, writes)

        def STT(out, in0, scalar, in1, op0, op1, reads=(), writes=()):
            return S.op("dve", lambda e: e.scalar_tensor_tensor(out, in0, scalar, in1, op0, op1), reads, writes)

        def CP(eng, out, in_, reads=(), writes=()):
            if eng == "act":
                return S.op("act", lambda e: e.copy(out, in_), reads, writes)
            return S.op(eng, lambda e: e.tensor_copy(out, in_), reads, writes)

        def MEMSET(eng, ap, val, writes=()):
            return S.op(eng, lambda e: e.memset(ap, val), (), writes)

        ring_state = {"issued": 0, "next": 0, "plan": list(plan_in) if plan_in is not None else [], "rec": []}

        def ring_issue_upto(k):
            plan = ring_state["plan"]
            while ring_state["issued"] < min(k, len(plan)):
                f = ring_state["issued"]
                slot = f % NSLOT
                src = wall[plan[f]].rearrange("p (a b) -> p a b", b=2048)
                dst = ring[slot].rearrange("p (a b) -> p a b", b=2048)
                DMA("pool", dst, src, "d_ring%d" % slot, writes=[("ring", slot)])
                ring_state["issued"] += 1

        def ring_next(idx):
            f = ring_state["next"]
            ring_state["next"] += 1
            ring_state["rec"].append(idx)
            if plan_in is None:
                ring_state["plan"].append(idx)
            else:
                assert ring_state["plan"][f] == idx
            ring_issue_upto(f + NSLOT)
            slot = f % NSLOT
            return ring[slot].rearrange("p (a b) -> p a b", b=128), ("ring", slot)

        pb = {"i": 0}

        def next_bank(lo=0, hi=8):
            b = lo + (pb["i"] % (hi - lo))
            pb["i"] += 1
            return b

        ld_rr = {"i": 0}

        DMA("sp", vecs, vecs_d, "d_c0", writes=["vecs"])
        DMA("sp", cond, cond_d, "d_c1", writes=["cond"])
        DMA("pool", cm.rearrange("p a b -> p (a b)"), cmat_d, "d_c2", writes=["cm"])
        ACT(scb, cond, AF.Silu, reads=["cond"], writes=["scb"])
        def mod_steps(l):
            mb = 7
            for nb in range(96):
                if nb % 2 == 0:
                    slot, rkey = ring_next(l * NS_LAYER + nb // 2)
                for kc in range(NCH):
                    MM(bank(mb)[:, nb:nb + 1], slot[:, (nb % 2) * 16 + kc, :], scb[:, kc:kc + 1],
                       kc == 0, kc == NCH - 1, reads=[rkey, "scb", "cm"], writes=[("ps", mb)])
                if nb % 2 == 1:
                    yield nb
            TT_("dve", mods[:, l, :], bank(mb)[:, 0:96], vecs[:, l * V_LAYER + V_MODB:l * V_LAYER + V_MODB + 96],
                ALU.add, reads=[("ps", mb), "vecs"], writes=[("mods", l)])

        for _ in mod_steps(0):
            pass

        PREFIX = 0

        def layer(l):
            vb = l * V_LAYER
            x_in = xT if l == 0 else xsA

            def vcol(off, n=1):
                return vecs[:, vb + off:vb + off + n]

            def mod(j):
                return mods[:, l, j * 16:(j + 1) * 16]

            STT(gs1, mod(1), 1.0, vcol(V_NMIX, 16), ALU.add, ALU.mult, reads=[("mods", l), "vecs"], writes=["gs1"])
            STT(gs2, mod(4), 1.0, vcol(V_NFFN, 16), ALU.add, ALU.mult, reads=[("mods", l), "vecs"], writes=["gs2"])
            ACT(lam8, vcol(V_LAM, 8), AF.Sigmoid, reads=["vecs"], writes=["lam8a"])
            ACT(lam8, lam8, AF.Ln, reads=["lam8a"], writes=["lam8b"])
            TS("dve", lam8, lam8, 8.0, None, ALU.mult, reads=["lam8b"], writes=["lam8"])
            CP("dve", cwq, vcol(V_CONVW, 16), reads=["vecs"], writes=["cwq"])
            TS("dve", sm[:, 40:41], vecs[:, V_FLAG:V_FLAG + 1], -1.0, None, ALU.add, reads=["vecs"], writes=["fm1"])
            TS("dve", cwe, cwq, sm[:, 40:41], None, ALU.mult, reads=["cwq", "fm1"], writes=["cwe"])

            S.barrier()
            ar.reset(0)
            MX = ar.bf16(16, T)
            kT = ar.bf16(NKV, 18 * 128)
            vS = ar.bf16(18, NKV, 132)
            lx = ar.bf16(4, T)
            prefix_off = ar.off
            cosT = ar.bf16(T)
            sinT = ar.bf16(T)
            hTs = [ar.bf16(NCH, TT) for _ in range(2)]
            stg = [ar.f32(TT) for _ in range(4)]
            sq = [ar.bf16(TT) for _ in range(2)]
            tmpn = [ar.f32(TT) for _ in range(2)]
            rstd = ar.f32(TT)
            qb16 = [ar.bf16(TT) for _ in range(2)]
            qc = [ar.f32(TT) for _ in range(2)]
            qs1 = ar.f32(TT)
            qs = [qs1, qs1]
            kf1 = ar.f32(TT)
            kf = [kf1, kf1]
            vf = [ar.f32(256) for _ in range(2)]

            DMA("pool", cosT, cos_d, "d_c3", writes=["cos"])
            DMA("pool", sinT, sin_d, "d_c4", writes=["sin"])
            MEMSET("pool", kT[:, :, 0:128], 0.0, writes=[("kT", 0, -1), ("kT", 1, -1)])
            MEMSET("pool", kT[:, :, 17 * 128:18 * 128], 0.0, writes=[("kT", 0, 16), ("kT", 1, 16)])
            MEMSET("pool", vS[:, 0, :, :], 0.0, writes=[("vS", -1)])
            MEMSET("pool", vS[:, 17, :, :], 0.0, writes=[("vS", 16)])
            S.op("pool", lambda e: e.memset(vS[:, :, :, 128:129], 1.0), (), ["vones"] + [("vS", b_) for b_ in range(-1, 17)])

            def xload(c, tt, k):
                return DMA("sp", stg[k], x_in[c * P:(c + 1) * P, tt * TT:(tt + 1) * TT], "d_ld%d" % k,
                           reads=[("x", l, c, tt)], writes=[("stg", k)])

            SSB = 7

            def norm_steps(tt, hbuf, hk):
                for c in range(min(3, NCH)):
                    xload(c, tt, c % 4)
                for c in range(NCH):
                    if c + 3 < NCH:
                        xload(c + 3, tt, (c + 3) % 4)
                    ACT(sq[c % 2], stg[c % 4], AF.Square, reads=[("stg", c % 4)], writes=[("sq", c % 2)])
                    MM(bank(SSB), ones_b, sq[c % 2], c == 0, c == NCH - 1,
                       reads=[("sq", c % 2), "cm"], writes=[("ps", SSB)])
                    yield c
                ACT(rstd, bank(SSB), AF.Sqrt, reads=[("ps", SSB)], writes=["rstd_a"], bias=EPS, scale=1.0 / D)
                S.op("dve", lambda e: e.reciprocal(rstd, rstd), ["rstd_a"], ["rstd"])
                yield -1
                for c in range(min(3, NCH)):
                    xload(c, tt, c % 4)
                for c in range(NCH):
                    if c + 3 < NCH:
                        xload(c + 3, tt, (c + 3) % 4)
                    STT(tmpn[c % 2], stg[c % 4], gs1[:, c:c + 1], rstd, ALU.mult, ALU.mult,
                        reads=[("stg", c % 4), "gs1", "rstd"], writes=[("tmpn", c % 2)])
                    ACT(hbuf[:, c, :], tmpn[c % 2], AF.Identity, reads=[("tmpn", c % 2), ("mods", l)],
                        writes=[("hT", hk, c)], bias=mod(0)[:, c:c + 1])
                    yield c

            for _ in norm_steps(0, hTs[0], 0):
                pass
            for tt in range(NTT):
                tsl = slice(tt * TT, (tt + 1) * TT)
                hT = hTs[tt % 2]
                hk = tt % 2
                ngen = norm_steps(tt + 1, hTs[(tt + 1) % 2], (tt + 1) % 2) if tt + 1 < NTT else None

                def weave(n=2):
                    if ngen is not None:
                        for _ in range(n):
                            next(ngen, None)

                hreads = [("hT", hk, c) for c in range(NCH)]
                slot = rkey = None
                pend_a = [None]
                for cb in range(24):
                    if cb % 2 == 0:
                        slot, rkey = ring_next(l * NS_LAYER + NS_MOD + cb // 2)
                    if cb == 10:
                        for bi in range(4):
                            blk = tt * 4 + bi
                            b_ = next_bank(0, 7)
                            for kc in range(NCH):
                                rhs = slot[:, kc:kc + 17:16, :]
                                MM(bank(b_)[:, 0:256], hT[:, kc, bi * P:(bi + 1) * P], rhs, kc == 0, kc == NCH - 1,
                                   reads=[rkey] + (hreads if kc == 0 else []), writes=[("ps", b_)])
                            if pend_a[0] is not None:
                                pend_a[0]()
                                pend_a[0] = None
                            k2 = blk % 2
                            CP("act", vf[k2], bank(b_)[:, 0:256], reads=[("ps", b_)], writes=[("vf", k2)])
                            CP("dve", vS[:, blk + 1, :, 0:128], bank(b_)[:, 0:256].rearrange("p (a b) -> p a b", a=2),
                               reads=[("ps", b_)], writes=[("vS", blk)])
                            DMA("act", v_o[l, blk * P:(blk + 1) * P, :], vf[k2], "d_vo%d" % k2, reads=[("vf", k2)])
                            weave()
                        continue
                    if cb == 11:
                        continue
                    b_ = next_bank(0, 7)
                    for kc in range(NCH):
                        MM(bank(b_), slot[:, (cb % 2) * 16 + kc, :], hT[:, kc, :], kc == 0, kc == NCH - 1,
                           reads=[rkey] + (hreads if kc == 0 else []), writes=[("ps", b_)])
                    if pend_a[0] is not None:
                        pend_a[0]()
                        pend_a[0] = None
                    if cb < 10:
                        i2 = cb % 2
                        CP("act", qb16[i2], bank(b_), reads=[("ps", b_)], writes=[("qb16", i2)])
                        TT_("dve", qc[i2], bank(b_), cosT[:, tsl], ALU.mult, reads=[("ps", b_), "cos"], writes=[("qc", i2)])

                        def rope_tail(cb=cb, i2=i2, tt=tt, tsl=tsl):
                            b2 = next_bank(0, 7)
                            MM(bank(b2), rot_b, qb16[i2], True, True, reads=[("qb16", i2), "cm"], writes=[("ps", b2)])
                            TT_("dve", qs[i2], bank(b2), sinT[:, tsl], ALU.mult, reads=[("ps", b2), "sin"], writes=[("qs", 0)])
                            if cb < 8:
                                TT_("dve", MX[:, cb, tsl], qc[i2], qs[i2], ALU.add,
                                    reads=[("qc", i2), ("qs", 0)], writes=[("mx", cb, tt)])
                            else:
                                kv = cb - 8
                                TT_("dve", kf[kv], qc[i2], qs[i2], ALU.add,
                                    reads=[("qc", i2), ("qs", 0)], writes=[("kf", 0)])
                                CP("act", kT[:, kv, P + tt * TT:P + (tt + 1) * TT], kf[kv], reads=[("kf", 0)],
                                   writes=[("kT", kv, tt * 4 + j) for j in range(4)])
                                DMA("act", kT_o[l, kv, :, tsl], kf[kv], "d_ko0", reads=[("kf", 0)])

                        pend_a[0] = rope_tail
                    elif cb < 16:
                        CP("act", lx[:, cb - 12, tsl], bank(b_), reads=[("ps", b_)], writes=[("lx", cb - 12, tt)])
                    elif cb < 20:
                        ACT(MX[:, 8 + cb - 16, tsl], bank(b_), AF.Gelu_apprx_tanh, reads=[("ps", b_)],
                            writes=[("mx", 8 + cb - 16, tt)])
                    else:
                        CP("dve", MX[:, 12 + cb - 20, tsl], bank(b_), reads=[("ps", b_)], writes=[("mx", 12 + cb - 20, tt)])
                    weave()
                if ngen is not None:
                    for _ in ngen:
                        pass

            chk('A', l)
            S.barrier()
            ar.reset(prefix_off)
            biasb = ar.bf16(NBK, 384)
            ckT = ar.bf16(NKV, 512)
            cvS = ar.bf16(4, NKV, 132)
            pint = ar.bf16(4, 112)
            pedge = ar.bf16(NBK * 4 * 4, 8)
            poolw = ar.bf16(4, 128)
            zS = ar.bf16(NBK, 4 * 128)
            Sb = [ar.f32(384) for _ in range(2)]
            Pb = [ar.bf16(896) for _ in range(2)]
            PT = [ar.bf16(7, 128) for _ in range(2)]
            Osb = [ar.bf16(128) for _ in range(2)]
            DMA("pool", biasb.rearrange("p a b -> p (a b)").rearrange("p (a b) -> p a b", b=2048),
                bias_d.rearrange("p (a b) -> p a b", b=2048), "d_c5", writes=["biasb"])
            DMA("pool", ckT.rearrange("p a b -> p (a b)"), ckT_d[l], "d_c6", writes=["ckT"])
            for b4 in range(4):
                DMA("pool", cvS[:, b4, :, 0:128], cv_d[l, b4 * P:(b4 + 1) * P, :].rearrange("p (k d) -> p k d", k=NKV),
                    "d_c7", writes=[("cvSb", b4)])
            S.op("pool", lambda e: e.memset(cvS[:, :, :, 128:129], 1.0), (), ["cvones"])
            DMA("pool", pint.rearrange("p a b -> p (a b)"), pint_d, "d_c8", writes=["pint"])
            DMA("pool", pedge.rearrange("p a b -> p (a b)").rearrange("p (a b) -> p a b", b=2048),
                pedge_d.rearrange("p (a b) -> p a b", b=2048), "d_c9", writes=["pedge"])
            DMA("pool", poolw.rearrange("p a b -> p (a b)"), poolw_d[l], "d_c10", writes=["poolw"])

            items = [(qb, h) for qb in range(NBK) for h in range(NH)]
            nit = len(items)
            SP_ = [(0, 1), (2, 3)]
            PTB = 4
            OBS = [5, 6]
            OTB = 7
            st = vecs[:, vb + V_SINK:vb + V_SINK + 8]
            ctxb = vecs[:, V_CTXB:V_CTXB + 1]

            def smv(i, j):
                c0 = 8 * (i % 4) + j
                return sm[:, c0:c0 + 1]

            def phaseA(i):
                qb, h = items[i]
                kv = h // 4
                k2 = i % 2
                k4 = i % 4
                b0, b1 = SP_[k2]
                Sp = ps_t[:, b0 * 512:(b1 + 1) * 512]
                q_ap = MX[:, h, qb * P:(qb + 1) * P]
                MM(Sp[:, 128:512], q_ap, kT[:, kv, qb * P:(qb + 3) * P], True, True,
                   reads=[("mx", h, qb)] + [("kT", kv, j) for j in (qb - 1, qb, qb + 1)],
                   writes=[("ps", b0), ("ps", b1)])
                MM(Sp[:, 512:1024], q_ap, ckT[:, kv, :], True, True, reads=["ckT"], writes=[("ps", b1)])
                mx_, nb_, nb2, es_ = smv(i, 0), smv(i, 1), smv(i, 2), smv(i, 3)
                S.op("dve", lambda e: e.reduce_max(mx_, Sp[:, 128:1024], AX.X), [("ps", b0), ("ps", b1)], [("mx_", k4)])
                TS("dve", nb_, mx_, -SCALE, None, ALU.mult, reads=[("mx_", k4)], writes=[("nb", k4)])
                TS("dve", nb2, mx_, -SCALE, ctxb, ALU.mult, ALU.add, reads=[("mx_", k4), "vecs"], writes=[("nb2", k4)])
                TT_("dve", Sb[k2], Sp[:, 128:512], biasb[:, qb, :], ALU.add, reads=[("ps", b0), "biasb"], writes=[("Sb", k2)])
                ACT(Pb[k2][:, 0:384], Sb[k2], AF.Exp, reads=[("Sb", k2), ("nb", k4)], writes=[("Pb", k2)],
                    bias=nb_, scale=SCALE)
                ACT(Pb[k2][:, 384:896], Sp[:, 512:1024], AF.Exp, reads=[("ps", b1), ("nb2", k4)], writes=[("Pb", k2)],
                    bias=nb2, scale=SCALE)
                ACT(es_, st[:, h:h + 1], AF.Exp, reads=["vecs", ("nb", k4)], writes=[("es", k4)], bias=nb_, scale=1.0)

            def phaseT(i):
                k2 = i % 2
                ptp = bank_bf(PTB)
                for j in range(7):
                    TR(ptp[:, j * P:(j + 1) * P], Pb[k2][:, j * P:(j + 1) * P], reads=[("Pb", k2), "cm"], writes=[("ps", PTB)])
                CP("act" if i % 3 else "dve", PT[k2].rearrange("p a b -> p (a b)"), ptp[:, 0:896], reads=[("ps", PTB)], writes=[("PT", k2)])

            def phaseV(i):
                qb, h = items[i]
                kv = h // 4
                k2 = i % 2
                k4 = i % 4
                ob = OBS[k2]
                O = bank(ob)[:, 0:129]
                for j in range(7):
                    if j < 3:
                        rhs = vS[:, qb + j, kv, 0:129]
                        rd = [("vS", qb + j - 1), "vones"]
                    else:
                        rhs = cvS[:, j - 3, kv, 0:129]
                        rd = [("cvSb", j - 3), "cvones"]
                    MM(O, PT[k2][:, j, :], rhs, j == 0, j == 6, reads=[("PT", k2)] + rd, writes=[("ps", ob)])
                den, es_ = smv(i, 4), smv(i, 3)
                TT_("dve", den, bank(ob)[:, 128:129], es_, ALU.add, reads=[("ps", ob), ("es", k4)], writes=[("den", k4)])
                S.op("dve", lambda e: e.reciprocal(den, den), [("den", k4)], [("rden", k4)])
                ACT(Osb[k2], bank(ob)[:, 0:128], AF.Copy, reads=[("ps", ob), ("rden", k4)], writes=[("Osb", k2)], scale=den)

            def phaseC(i):
                qb, h = items[i]
                k2 = i % 2
                otp = bank_bf(OTB)
                TR(otp[:, 0:128], Osb[k2], reads=[("Osb", k2), "cm"], writes=[("ps", OTB)])
                CP("dve", MX[:, h, qb * P:(qb + 1) * P], otp[:, 0:128], reads=[("ps", OTB)], writes=[("mx", h, qb)])

            for i in range(nit + 3):
                if i < nit:
                    phaseA(i)
                if 0 <= i - 1 < nit:
                    phaseT(i - 1)
                if 0 <= i - 2 < nit:
                    phaseV(i - 2)
                if 0 <= i - 3 < nit:
                    phaseC(i - 3)

            chk('attn', l)
            pb["i"] = 0
            for bi in range(NBK):
                b_ = next_bank(0, 4)
                for g in range(4):
                    MM(bank(b_)[:, g * P:(g + 1) * P], MX[:, 12 + g, bi * P:(bi + 1) * P], poolw[:, g, :], True, True,
                       reads=["poolw", ("mx", 12 + g, bi)], writes=[("ps", b_)])
                CP("act" if bi % 2 else "dve", zS[:, bi, :], bank(b_), reads=[("ps", b_)], writes=[("zS", bi)])
            pe4 = pedge.rearrange("p (i g k) e -> p i g k e", i=NBK, g=4, k=4)
            for g in range(4):
                for gi in range(4):
                    b_ = next_bank(4, 8)
                    for j in range(4):
                        bi = gi * 4 + j
                        o = bank(b_)[:, j * P:(j + 1) * P]
                        zi = zS[:, bi, g * P:(g + 1) * P]
                        MM(o[:, 8:120], zi, pint[:, g, :], True, True, reads=[("zS", bi), "pint"], writes=[("ps", b_)])
                        zp = zS[:, max(bi - 1, 0), g * P:(g + 1) * P]
                        zn = zS[:, min(bi + 1, NBK - 1), g * P:(g + 1) * P]
                        MM(o[:, 0:8], zp, pe4[:, bi, g, 0, :], True, False,
                           reads=[("zS", max(bi - 1, 0)), "pedge"], writes=[("ps", b_)])
                        MM(o[:, 0:8], zi, pe4[:, bi, g, 1, :], False, True, writes=[("ps", b_)])
                        MM(o[:, 120:128], zi, pe4[:, bi, g, 2, :], True, False, writes=[("ps", b_)])
                        MM(o[:, 120:128], zn, pe4[:, bi, g, 3, :], False, True,
                           reads=[("zS", min(bi + 1, NBK - 1))], writes=[("ps", b_)])
                    TS("dve", MX[:, 12 + g, gi * TT:(gi + 1) * TT], bank(b_), vcol(V_PSC + g), None, ALU.mult,
                       reads=[("ps", b_), "vecs"], writes=[("mx", 12 + g, "o", gi)])

            chk('B1', l)
            S.barrier()
            ar.reset(prefix_off)
            lw = ar.bf16(16, 128)
            u = ar.f32(T)
            ub = ar.bf16(T)
            aas = [ar.f32(T) for _ in range(2)]
            bbs = [ar.f32(T) for _ in range(2)]
            hf_ = ar.f32(T)
            hb_ = ar.f32(T)
            hh = [hf_, hb_]
            fin = ar.f32(2, 4, 8)
            dg = ar.bf16(4, 128)
            DMA("pool", lw.rearrange("p a b -> p (a b)"), lruw_d[l], "d_c5", writes=["lw"])
            flag = vecs[:, V_FLAG:V_FLAG + 1]

            def seg(ap):
                return ap.rearrange("p (s t) -> p s t", t=256)

            for n in range(4):
                lxn = lx[:, n, :]
                w = lambda j: vcol(V_CONVW + j * 4 + n)
                we = lambda j: cwe[:, j * 4 + n:j * 4 + n + 1]
                for j in range(4):
                    TS("dve", dg[:, j, :], ident_b, cwq[:, j * 4 + n:j * 4 + n + 1], None, ALU.mult,
                       reads=["cwq", "cm"], writes=[("dg", j)])
                for tt in range(NTT):
                    b_ = next_bank()
                    for jj, j in enumerate((2, 0, 1, 3)):
                        sh = j - 2
                        lo, hi = tt * TT + sh, tt * TT + sh + TT
                        olo, ohi = 0, TT
                        if lo < 0:
                            olo, lo = -lo, 0
                        if hi > T:
                            ohi, hi = TT - (hi - T), T
                        MM(bank(b_)[:, olo:ohi], dg[:, j, :], lxn[:, lo:hi], jj == 0, jj == 3,
                           reads=[("dg", j)], writes=[("ps", b_)])
                    ACT(u[:, tt * TT:(tt + 1) * TT], bank(b_), AF.Identity, reads=[("ps", b_), "vecs", "hsum"],
                        writes=[("u0", tt)], bias=vcol(V_CONVB + n))
                us, ls = seg(u), seg(lxn)
                STT(us[:, 1:8, 0:2], ls[:, 0:7, 254:256], we(0), us[:, 1:8, 0:2], ALU.mult, ALU.add,
                    reads=[("u0", t_) for t_ in range(NTT)] + ["cwe"], writes=["u4"])
                STT(us[:, 1:8, 0:1], ls[:, 0:7, 255:256], we(1), us[:, 1:8, 0:1], ALU.mult, ALU.add, reads=["u4"], writes=["u5"])
                STT(us[:, 0:7, 255:256], ls[:, 1:8, 0:1], we(3), us[:, 0:7, 255:256], ALU.mult, ALU.add,
                    reads=["u5"], writes=["u"])
                CP("act", ub, u, reads=["u"], writes=["ub"])
                for d in range(2):
                    for tt in range(NTT):
                        tsl = slice(tt * TT, (tt + 1) * TT)
                        b1_ = next_bank()
                        MM(bank(b1_), lw[:, d * 4 + n, :], ub[:, tsl], True, True, reads=["lw", "ub"], writes=[("ps", b1_)])
                        b2_ = next_bank()
                        MM(bank(b2_), lw[:, 8 + d * 4 + n, :], ub[:, tsl], True, True, reads=["lw", "ub"], writes=[("ps", b2_)])
                        ACT(aas[d][:, tsl], bank(b1_), AF.Sigmoid, reads=[("ps", b1_), "vecs", ("scan", d)],
                            writes=[("aa", d, tt)], bias=vcol(V_BA + d * 4 + n))
                        ACT(bbs[d][:, tsl], bank(b2_), AF.Sigmoid, reads=[("ps", b2_), "vecs", ("scan", d)],
                            writes=[("bb", d, tt)], bias=vcol(V_BX + d * 4 + n))
                for d in range(2):
                    ACT(aas[d], aas[d], AF.Exp, reads=[("aa", d, t_) for t_ in range(NTT)] + ["lam8"], writes=[("aE", d)],
                        scale=lam8[:, d * 4 + n:d * 4 + n + 1])
                    TT_("dve", hh[d], aas[d], aas[d], ALU.mult, reads=[("aE", d), "hsum"], writes=[("a2", d)])
                for d in range(2):
                    ACT(hh[d], hh[d], AF.Sqrt, reads=[("a2", d)], writes=[("sq", d)], bias=1.0, scale=-1.0)
                for d in range(2):
                    TT_("dve", bbs[d], bbs[d], hh[d], ALU.mult, reads=[("bb", d, t_) for t_ in range(NTT)] + [("sq", d)],
                        writes=[("b1", d)])
                    TT_("dve", bbs[d], bbs[d], u, ALU.mult, reads=[("b1", d), "u"], writes=[("b2", d)])
                    asg = seg(aas[d])
                    h0 = vcol(V_H0 + d * 4 + n)
                    if d == 0:
                        TS("dve", asg[:, 1:8, 0:1], asg[:, 1:8, 0:1], flag, None, ALU.mult, reads=[("aE", 0), "vecs"], writes=[("aam", 0)])
                        S.op("dve", lambda e, h0=h0: e.tensor_tensor_scan(hf_, aas[0], bbs[0], h0, ALU.mult, ALU.add),
                             [("aam", 0), "vecs", ("b2", 0), ("sq", 0)], [("scan", 0), "hf"])
                        CP("act", fin[:, 0, n, :], seg(hf_)[:, :, 255], reads=["hf"], writes=[("fin", 0, n)])
                    else:
                        TS("dve", asg[:, 0:7, 255:256], asg[:, 0:7, 255:256], flag, None, ALU.mult,
                           reads=[("aE", 1), "vecs"], writes=[("aam", 1)])
                        S.op("dve", lambda e, h0=h0: e.tensor_tensor_scan(hb_[:, ::-1], aas[1][:, ::-1], bbs[1][:, ::-1], h0,
                                                                          ALU.mult, ALU.add),
                             [("aam", 1), "vecs", ("b2", 1), ("sq", 1)], [("scan", 1), "hb"])
                        CP("act", fin[:, 1, n, :], seg(hb_)[:, :, 0], reads=["hb"], writes=[("fin", 1, n)])
                TT_("dve", hf_, hf_, hb_, ALU.add, reads=["hf", "hb", ("fin", 0, n), ("fin", 1, n)], writes=["hsum0"])
                TT_("dve", MX[:, 8 + n, :], hf_, MX[:, 8 + n, :], ALU.mult, reads=["hsum0"], writes=["hsum", ("mxl", n)])
            for d in range(2):
                DMA("sp", st_o[l, d].rearrange("n p s -> p n s"), fin[:, d, :, :], "d_so",
                    reads=[("fin", d, n) for n in range(4)])

            chk('B2', l)
            S.barrier()
            ar.reset(prefix_off)
            rstd2 = ar.f32(T)
            xin_ = [ar.f32(TT) for _ in range(4)]
            xo = [ar.f32(TT) for _ in range(4)]
            sq2 = [ar.bf16(TT) for _ in range(2)]
            for hf in range(2):
                ssb = [6, 7]
                steps = [(cb, t2) for cb in range(NCH) for t2 in range(2)]

                def cload(si):
                    cb, t2 = steps[si]
                    tt = hf * 2 + t2
                    k = si % 4
                    DMA("sp", xin_[k], x_in[cb * P:(cb + 1) * P, tt * TT:(tt + 1) * TT], "d_ld%d" % k,
                        reads=[("x", l, cb, tt)], writes=[("xin", k)])

                for si in range(3):
                    cload(si)
                slot = rkey = None
                pend_c = [None]
                for si, (cb, t2) in enumerate(steps):
                    if si + 3 < len(steps):
                        cload(si + 3)
                    if t2 == 0 and cb % 2 == 0:
                        slot, rkey = ring_next(l * NS_LAYER + NS_MOD + NS_IN + cb // 2)
                    tt = hf * 2 + t2
                    tsl = slice(tt * TT, (tt + 1) * TT)
                    b_ = next_bank(0, 6)
                    for kc in range(NCH):
                        MM(bank(b_), slot[:, (cb % 2) * 16 + kc, :], MX[:, kc, tsl], kc == 0, kc == NCH - 1,
                           reads=[rkey], writes=[("ps", b_)])
                    k = si % 4
                    STT(xo[k], bank(b_), mod(2)[:, cb:cb + 1], xin_[k], ALU.mult, ALU.add,
                        reads=[("ps", b_), ("xin", k), ("mods", l)], writes=[("xo", k)])
                    DMA("act", xsB[cb * P:(cb + 1) * P, tsl], xo[k], "d_st%d" % k, reads=[("xo", k)], writes=[("xB", cb, tt)])
                    ACT(sq2[si % 2], xo[k], AF.Square, reads=[("xo", k)], writes=[("sq2", si % 2)])
                    if pend_c[0] is not None:
                        pend_c[0]()
                    pend_c[0] = (lambda si=si, t2=t2, cb=cb: MM(bank(ssb[t2]), ones_b, sq2[si % 2], cb == 0, cb == NCH - 1,
                                                               reads=[("sq2", si % 2), "cm"], writes=[("ps", ssb[t2])]))
                pend_c[0]()
                pend_c[0] = None
                for t2 in range(2):
                    tt = hf * 2 + t2
                    tsl = slice(tt * TT, (tt + 1) * TT)
                    ACT(rstd2[:, tsl], bank(ssb[t2]), AF.Sqrt, reads=[("ps", ssb[t2])], writes=[("r2a", tt)],
                        bias=EPS, scale=1.0 / D)
                    S.op("dve", lambda e, tsl=tsl: e.reciprocal(rstd2[:, tsl], rstd2[:, tsl]), [("r2a", tt)], [("r2", tt)])

            chk('C', l)
            S.barrier()
            ar.reset(prefix_off + T)
            HT = 2 * TT
            stg2 = [ar.f32(TT) for _ in range(4)]
            tmp2 = [ar.f32(TT) for _ in range(2)]
            sg = [ar.f32(TT) for _ in range(2)]
            xo2 = [ar.f32(TT) for _ in range(4)]
            sq3 = [ar.bf16(TT) for _ in range(2)]
            stg3 = [ar.f32(TT) for _ in range(4)]
            end_small = ar.off
            ar.reset(0)
            h2 = ar.bf16(NCH, HT)
            actb_a = None
            if ar.off + (NFF * HT) // 2 <= prefix_off:
                actb = ar.bf16(NFF, HT)
            else:
                n_pre = (prefix_off - ar.off) * 2 // HT
                act_pre = ar.bf16(n_pre, HT)
                ar.reset(end_small)
                act_post = ar.bf16(NFF - n_pre, HT)
                actb = None
            if actb is None:
                def act_at(j):
                    return act_pre[:, j, :] if j < n_pre else act_post[:, j - n_pre, :]
            else:
                def act_at(j):
                    return actb[:, j, :]
            last = (l == nl - 1)
            x_out = xsA
            modgen = mod_steps(l + 1) if not last else None

            def mod_step():
                nonlocal modgen
                if modgen is not None:
                    try:
                        next(modgen)
                    except StopIteration:
                        modgen = None
            def h2_steps(hf):
                steps = [(c, t2) for t2 in range(2) for c in range(NCH)]

                def dload(si):
                    c, t2 = steps[si]
                    tt = hf * 2 + t2
                    k = si % 4
                    DMA("sp", stg3[k], xsB[c * P:(c + 1) * P, tt * TT:(tt + 1) * TT], "d_h%d" % k,
                        reads=[("xB", c, tt)], writes=[("stg3", k)])

                for si in range(3):
                    dload(si)
                for si, (c, t2) in enumerate(steps):
                    if si + 3 < len(steps):
                        dload(si + 3)
                    tt = hf * 2 + t2
                    k = si % 4
                    STT(tmp2[si % 2], stg3[k], gs2[:, c:c + 1], rstd2[:, tt * TT:(tt + 1) * TT], ALU.mult, ALU.mult,
                        reads=[("stg3", k), "gs2", ("r2", tt)], writes=[("tmp2", si % 2)])
                    ACT(h2[:, c, t2 * TT:(t2 + 1) * TT], tmp2[si % 2], AF.Identity,
                        reads=[("tmp2", si % 2), ("mods", l)], writes=[("h2", c, t2)], bias=mod(3)[:, c:c + 1])
                    yield si

            h2gen1 = h2_steps(1)
            for hf in range(2):
                if hf == 0:
                    for _ in h2_steps(0):
                        pass
                else:
                    for _ in h2gen1:
                        pass
                for j in range(NFF):
                    slot, rkey = ring_next(l * NS_LAYER + NS_MOD + NS_IN + NS_OUT + j)
                    for t2 in range(2):
                        bg = next_bank(0, 6)
                        bu = next_bank(0, 6)
                        hr = [("h2", c, t2) for c in range(NCH)]
                        for kc in range(NCH):
                            MM(bank(bg), slot[:, kc, :], h2[:, kc, t2 * TT:(t2 + 1) * TT], kc == 0, kc == NCH - 1,
                               reads=[rkey] + (hr if kc == 0 else []), writes=[("ps", bg)])
                        for kc in range(NCH):
                            MM(bank(bu), slot[:, 16 + kc, :], h2[:, kc, t2 * TT:(t2 + 1) * TT], kc == 0, kc == NCH - 1,
                               reads=[rkey], writes=[("ps", bu)])
                        k2 = (j * 2 + t2) % 2
                        ACT(sg[k2], bank(bg), AF.Silu, reads=[("ps", bg)], writes=[("sg", k2)])
                        TT_("dve", act_at(j)[:, t2 * TT:(t2 + 1) * TT], bank(bu), sg[k2], ALU.mult,
                            reads=[("ps", bu), ("sg", k2)], writes=[("act", j, t2)])
                    if j % 2 == 1 or j in (0, 10, 20):
                        mod_step()
                steps = [(m, t2) for m in range(NCH) for t2 in range(2)]

                def eload(si):
                    m, t2 = steps[si]
                    tt = hf * 2 + t2
                    k = si % 4
                    DMA("sp", stg2[k], xsB[m * P:(m + 1) * P, tt * TT:(tt + 1) * TT], "d_ld%d" % k,
                        reads=[("xB", m, tt)], writes=[("stg2", k)])

                for si in range(3):
                    eload(si)
                ssb = [6, 7]
                cur_s = {"idx": -1, "slot": None, "rkey": None}
                pend_d = []
                for m in range(NCH + 1):
                    if m == NCH:
                        for f_ in pend_d:
                            f_()
                        pend_d = []
                        break
                    bks = [next_bank(0, 6), next_bank(0, 6)]
                    for kc in range(NFF):
                        if kc == 8 and pend_d:
                            for f_ in pend_d:
                                f_()
                            pend_d = []
                        bidx = m * NFF + kc
                        sidx = bidx // 32
                        if sidx != cur_s["idx"]:
                            cur_s["slot"], cur_s["rkey"] = ring_next(l * NS_LAYER + NS_MOD + NS_IN + NS_OUT + NS_W1 + sidx)
                            cur_s["idx"] = sidx
                        slot, rkey = cur_s["slot"], cur_s["rkey"]
                        for t2 in range(2):
                            MM(bank(bks[t2]), slot[:, bidx % 32, :], act_at(kc)[:, t2 * TT:(t2 + 1) * TT],
                               kc == 0, kc == NFF - 1, reads=[rkey, ("act", kc, t2)], writes=[("ps", bks[t2])])
                    for t2 in range(2):
                        si = m * 2 + t2
                        if si + 3 < len(steps):
                            eload(si + 3)
                        tt = hf * 2 + t2
                        tsl = slice(tt * TT, (tt + 1) * TT)
                        b_ = bks[t2]
                        k = si % 4
                        STT(xo2[k], bank(b_), mod(5)[:, m:m + 1], stg2[k], ALU.mult, ALU.add,
                            reads=[("ps", b_), ("stg2", k), ("mods", l)], writes=[("xo2", k)])
                        DMA("act", x_out[m * P:(m + 1) * P, tsl], xo2[k], "d_st%d" % k, reads=[("xo2", k)],
                            writes=[("x", l + 1, m, tt)])
                        if last:
                            ACT(sq3[si % 2], xo2[k], AF.Square, reads=[("xo2", k)], writes=[("sq3", si % 2)])
                            pend_d.append(lambda si=si, t2=t2, m=m: MM(bank(ssb[t2]), ones_b, sq3[si % 2], m == 0, m == NCH - 1,
                                                                       reads=[("sq3", si % 2), "cm"], writes=[("ps", ssb[t2])]))
                    if hf == 0:
                        for _ in range(2):
                            next(h2gen1, None)
                if last:
                    for t2 in range(2):
                        tt = hf * 2 + t2
                        tsl = slice(tt * TT, (tt + 1) * TT)
                        ACT(rstd2[:, tsl], bank(ssb[t2]), AF.Sqrt, reads=[("ps", ssb[t2])], writes=[("r3a", tt)],
                            bias=EPS, scale=1.0 / D)
                        S.op("dve", lambda e, tsl=tsl: e.reciprocal(rstd2[:, tsl], rstd2[:, tsl]), [("r3a", tt)], [("r3", tt)])
                    steps = [(c, t2) for t2 in range(2) for c in range(NCH)]

                    def fload(si):
                        c, t2 = steps[si]
                        tt = hf * 2 + t2
                        k = si % 4
                        DMA("sp", stg2[k], xsA[c * P:(c + 1) * P, tt * TT:(tt + 1) * TT], "d_ld%d" % k,
                            reads=[("x", l + 1, c, tt)], writes=[("stg2", k)])

                    for si in range(3):
                        fload(si)
                    for si, (c, t2) in enumerate(steps):
                        if si + 3 < len(steps):
                            fload(si + 3)
                        tt = hf * 2 + t2
                        tsl = slice(tt * TT, (tt + 1) * TT)
                        k = si % 4
                        STT(xo2[k], stg2[k], vecs[:, V_NFIN + c:V_NFIN + c + 1], rstd2[:, tsl], ALU.mult, ALU.mult,
                            reads=[("stg2", k), "vecs", ("r3", tt)], writes=[("xo2", k)])
                        DMA("sp", yT[c * P:(c + 1) * P, tsl], xo2[k], "d_st%d" % k, reads=[("xo2", k)])
            while modgen is not None:
                mod_step()

        def chk(tag, l=0):
            if STOP_AT == tag or STOP_AT == "%s%d" % (tag, l):
                raise _Stop()

        try:
            chk("pro")
            for l in range(nl):
                layer(l)
        except _Stop:
            pass
        S.barrier()
        final_waits = [(k, v) for k, v in S.dval.items()]
        S.stream["sp"].append((final_waits, None, None))

        def replay(eng_name):
            def run(e):
                for waits, fn, dsem in S.stream[eng_name]:
                    for key, val in waits:
                        e.wait_ge(semh[key], val)
                    if fn is None:
                        continue
                    ins = fn(e)
                    if dsem is not None:
                        ins.then_inc(semh[dsem], 16)
                    else:
                        ins.then_inc(semh[eng_name], 1)
            return run

        block.tensor(replay("pe"))
        block.vector(replay("dve"))
        block.scalar(replay("act"))
        block.gpsimd(replay("pool"))
        block.sync(replay("sp"))
    return nc, ring_state["rec"]


def _fm(v):
    v = np.asarray(v, np.float32).reshape(-1, P)
    return np.ascontiguousarray(v.T)


def _tile_w(w):
    K, N = w.shape
    kc, cb = K // P, N // P
    b = w.reshape(kc, P, cb, P).transpose(2, 0, 1, 3)
    b = b.reshape(cb * kc, P, P)
    nb = b.shape[0]
    assert nb % 32 == 0
    return np.ascontiguousarray(b.reshape(nb // 32, 32, P, P).transpose(0, 2, 1, 3).reshape(nb // 32, P, 32 * P))


def _tile_w1(w1):
    K, N = w1.shape
    ga = w1[:, :N // 2].reshape(NCH, P, NFF, P)
    up = w1[:, N // 2:].reshape(NCH, P, NFF, P)
    s = np.stack([ga, up], axis=0)
    s = s.transpose(3, 2, 0, 1, 4)
    return np.ascontiguousarray(s.reshape(NFF, P, 32 * P))


def _rope_tables(seg_len, rope):
    if not rope:
        return np.ones((P, T), np.float32), np.zeros((P, T), np.float32)
    t = np.arange(T)
    row = (t // 64).astype(np.float32)
    col = (t % 64).astype(np.float32)
    rd = HD // 4
    inv = (np.float32(10000.0) ** (-np.arange(rd, dtype=np.float32) / np.float32(rd))).astype(np.float32)
    ang = np.zeros((T, 2, 2, rd), np.float32)
    ang[:, 0, :, :] = (row[:, None] * inv[None, :])[:, None, :]
    ang[:, 1, :, :] = (col[:, None] * inv[None, :])[:, None, :]
    ang = ang.reshape(T, HD)
    return np.ascontiguousarray(np.cos(ang).T.astype(np.float32)), np.ascontiguousarray(np.sin(ang).T.astype(np.float32))


def _band_bias(sample):
    NEG = np.float32(-1e30)
    b = np.zeros((P, NBK, 384), np.float32)
    i = np.arange(P)[:, None]
    j = np.arange(P)[None, :]
    for qb in range(NBK):
        if sample:
            prev = np.where(j >= i, 0.0, NEG) if qb >= 1 else np.full((P, P), NEG)
            nxt = np.where(j <= i, 0.0, NEG) if qb <= NBK - 2 else np.full((P, P), NEG)
        else:
            prev = np.zeros((P, P)) if qb % 2 == 1 else np.full((P, P), NEG)
            nxt = np.zeros((P, P)) if qb % 2 == 0 else np.full((P, P), NEG)
        b[:, qb, 0:128] = prev
        b[:, qb, 256:384] = nxt
    return b.reshape(P, NBK * 384)


def _pool_tables(seg_len):
    wins = (2, 4, 8, 16)
    pint = np.zeros((P, 4, 112), np.float32)
    pedge = np.zeros((P, NBK, 4, 4, 8), np.float32)
    for g, w in enumerate(wins):
        M = np.zeros((T, T), np.float32)
        for s0 in range(0, T, seg_len):
            for tl in range(seg_len):
                lo = min(max(tl - w // 2, 0), seg_len)
                hi = min(max(tl + w // 2, 0), seg_len)
                to = s0 + tl
                M[s0 + lo:s0 + hi, to] = np.float32(1.0) / np.float32(hi - lo)
                M[to, to] -= 1.0
        blk = M[128:256, 128:256] if seg_len >= 384 else None
        for bi in range(NBK):
            d = M[bi * P:(bi + 1) * P, bi * P:(bi + 1) * P]
            if bi == 0:
                pint[:, g, :] = d[:, 8:120]
            else:
                assert np.array_equal(pint[:, g, :], d[:, 8:120])
            pedge[:, bi, g, 1, :] = d[:, 0:8]
            pedge[:, bi, g, 2, :] = d[:, 120:128]
            if bi > 0:
                pedge[:, bi, g, 0, :] = M[(bi - 1) * P:bi * P, bi * P:bi * P + 8]
            if bi < NBK - 1:
                pedge[:, bi, g, 3, :] = M[(bi + 1) * P:(bi + 2) * P, bi * P + 120:(bi + 1) * P]
    return pint.reshape(P, 4 * 112), pedge.reshape(P, NBK * 4 * 4 * 8)


def _cmat():
    ones = np.ones((P, P), np.float32)
    R = np.zeros((P, P), np.float32)
    for m in range(P):
        if (m // 32) % 2 == 0:
            R[m + 32, m] = -1.0
        else:
            R[m - 32, m] = 1.0
    I = np.eye(P, dtype=np.float32)
    return np.ascontiguousarray(np.concatenate([ones, R, I], axis=1))


def prepare_inputs(inp, nl=DEPTH):
    f = lambda k: np.asarray(inp[k], np.float32)
    x_prompt, x_sample = f("x_prompt"), f("x_sample")
    cache_k, cache_v, state_lru = f("cache_k"), f("cache_v"), f("state_lru")
    c, c_ctx = f("c"), f("c_ctx")
    slots = []
    for l in range(nl):
        slots.append(_tile_w(f("mod_w")[l]))
        slots.append(_tile_w(f("w_in")[l]))
        slots.append(_tile_w(f("w_out")[l]))
        slots.append(_tile_w1(f("ffn_w1")[l]))
        slots.append(_tile_w(f("ffn_w2")[l]))
    wall = np.concatenate(slots, axis=0)
    assert wall.shape[0] == nl * NS_LAYER, wall.shape
    lruw = np.zeros((DEPTH, P, 16, P), np.float32)
    wa, wx = f("lru_wa"), f("lru_wx")
    for l in range(DEPTH):
        for d in range(2):
            for n in range(4):
                lruw[l, :, d * 4 + n, :] = wa[l, d, n]
                lruw[l, :, 8 + d * 4 + n, :] = wx[l, d, n]
    lruw = lruw.reshape(DEPTH, P, 16 * P)
    poolw = np.ascontiguousarray(f("pool_w").transpose(0, 2, 1, 3)).reshape(DEPTH, P, 4 * P)
    cmat = _cmat()
    tabs = {}
    for kind, seg_len in (("p", 256), ("s", 2048)):
        cosT, sinT = _rope_tables(seg_len, kind == "s")
        pint, pedge = _pool_tables(seg_len)
        tabs[kind] = dict(cosT=cosT, sinT=sinT, biasb=_band_bias(kind == "s"), pint=pint, pedge=pedge)

    def vec_common():
        v = np.zeros((P, NV), np.float32)
        for l in range(DEPTH):
            b = l * V_LAYER
            v[:, b + V_MODB:b + V_MODB + 96] = _fm(f("mod_b")[l])
            v[:, b + V_NMIX:b + V_NMIX + 16] = _fm(f("norm_mix")[l])
            v[:, b + V_NFFN:b + V_NFFN + 16] = _fm(f("norm_ffn")[l])
            v[:, b + V_CONVW:b + V_CONVW + 16] = _fm(f("conv_w")[l])
            v[:, b + V_CONVB:b + V_CONVB + 4] = _fm(f("conv_b")[l])
            v[:, b + V_BA:b + V_BA + 8] = _fm(f("lru_ba")[l])
            v[:, b + V_BX:b + V_BX + 8] = _fm(f("lru_bx")[l])
            v[:, b + V_LAM:b + V_LAM + 8] = _fm(f("lru_lambda")[l])
            v[:, b + V_PSC:b + V_PSC + 4] = _fm(f("pool_scale")[l])
            v[:, b + V_SINK:b + V_SINK + 8] = f("attn_sink")[l][None, :]
        v[:, V_NFIN:V_NFIN + 16] = _fm(f("norm_final"))
        return v

    vbase = vec_common()
    zeros_ck = np.zeros((DEPTH, P, NKV * 512), np.float32)
    zeros_cv = np.zeros((DEPTH, 512, NKV * HD), np.float32)
    in_maps = []
    for ci in range(8):
        if ci in (4, 5):
            b = ci - 4
            kind = "s"
            x = x_sample[b]
            cond = c[b]
            v = vbase.copy()
            for l in range(DEPTH):
                v[:, l * V_LAYER + V_H0:l * V_LAYER + V_H0 + 8] = _fm(state_lru[b, l])
            v[:, V_FLAG] = 1.0
            v[:, V_CTXB] = 0.0
            ckT = np.ascontiguousarray(cache_k[b].transpose(0, 3, 2, 1)).reshape(DEPTH, P, NKV * 512)
            cv = np.ascontiguousarray(cache_v[b]).reshape(DEPTH, 512, NKV * HD)
        else:
            pc = ci if ci < 4 else ci - 6
            kind = "p"
            x = x_prompt[pc * 8:(pc + 1) * 8].reshape(T, D)
            cond = c_ctx
            v = vbase.copy()
            v[:, V_FLAG] = 0.0
            v[:, V_CTXB] = -1e30
            ckT, cv = zeros_ck, zeros_cv
        m = dict(xT=np.ascontiguousarray(x.T), cond=_fm(cond), vecs=v, wall=wall, lruw=lruw, poolw=poolw,
                 ckT=ckT, cv=cv, cmat=cmat)
        m.update(tabs[kind])
        in_maps.append(m)
    return in_maps


_PROG = {}


def kernel(**inputs):
    return run_step(inputs, DEPTH)


def run_step(inputs, nl, trace=False):
    in_maps = prepare_inputs(inputs, nl)
    if nl not in _PROG:
        _PROG[nl] = build_program(nl)
    nc = _PROG[nl]
    res = run_bass_kernel_spmd(nc, in_maps, core_ids=list(range(8)), **({'trace': True} if trace else {}))
    _PROG['last'] = res
    r = res.results
    B, SEQ = 32, 256
    y_prompt = np.zeros((B, SEQ, D), np.float32)
    new_k = np.zeros((B, DEPTH, SEQ, NKV, HD), np.float32)
    new_v = np.zeros((B, DEPTH, SEQ, NKV, HD), np.float32)
    new_s = np.zeros((B, DEPTH, 2, 512), np.float32)
    y_sample = np.zeros((2, T, D), np.float32)
    for ci in range(4):
        o = r[ci]
        sl = slice(ci * 8, (ci + 1) * 8)
        y_prompt[sl] = o["yT"].T.reshape(8, SEQ, D)
        new_k[sl] = o["kT_o"].reshape(DEPTH, NKV, HD, 8, SEQ).transpose(3, 0, 4, 1, 2)
        new_v[sl] = o["v_o"].reshape(DEPTH, 8, SEQ, NKV, HD).transpose(1, 0, 2, 3, 4)
        new_s[sl] = o["st_o"].transpose(4, 0, 1, 2, 3).reshape(8, DEPTH, 2, 512)
    for b in range(2):
        y_sample[b] = r[4 + b]["yT"].T
    return (y_prompt, y_sample, new_k, new_v, new_s)
```
